# Optimizing a Trainium2 kernel written in Bass

```python
import math
import jax
import jax.numpy as jnp
from jax import lax
import numpy as np

D_MODEL = 1024
BATCH = 2
SEQ = 8192
DEPTH = 4

GRID_W = 64
CTX_LEN = 256
N_BRANCH = 4
BRANCH_W = 256
ML_HEADS = 4
ML_DH = 64
ML_CHUNK = 128
CONV_W = 3
DA_HEADS = 4
DA_DK = 32
DA_DV = 64
Q_BLOCK = 128
ROPE_BASE = 10000.0
FN_GROUPS = 4
FN_GD = 64
SG_GROUPS = 4
SG_GD = 64
SG_CHUNK = 128
D_FF = 2816
EPS = 1e-6

COL_SIZES = (2 * ML_HEADS * ML_DH, ML_HEADS * ML_DH, ML_HEADS * ML_DH, 4 * ML_HEADS,
             2 * DA_HEADS * DA_DK, 2 * DA_HEADS * DA_DK, DA_HEADS * DA_DV,
             FN_GROUPS * FN_GD, SG_GROUPS * SG_GD, SG_GROUPS * SG_GD, N_BRANCH * D_MODEL)
IN_WIDTH = sum(COL_SIZES)

kernel_name = 'hybrid_mlstm_diffattn_fnet_sgu_dit_trunk'


def rmsnorm(x, g):
    xf = x.astype(jnp.float32)
    y = xf * lax.rsqrt(jnp.mean(xf * xf, axis=-1, keepdims=True) + EPS)
    return (y * g.astype(jnp.float32)).astype(x.dtype)


def layernorm(x, g):
    xf = x.astype(jnp.float32)
    xf = xf - jnp.mean(xf, axis=-1, keepdims=True)
    y = xf * lax.rsqrt(jnp.mean(xf * xf, axis=-1, keepdims=True) + EPS)
    return (y * g.astype(jnp.float32)).astype(x.dtype)


def dwconv(x, w, b):
    r = w.shape[0] // 2
    S = x.shape[1]
    xp = jnp.pad(x, ((0, 0), (r, r), (0, 0)))
    return sum(xp[:, j:j + S] * w[j] for j in range(w.shape[0])) + b


def split_cols(p):
    idx = []
    acc = 0
    for s in COL_SIZES[:-1]:
        acc += s
        idx.append(acc)
    return jnp.split(p, idx, axis=-1)


def rope_2d(x, rows, cols):
    n = DA_DK // 4
    freqs = ROPE_BASE ** (-jnp.arange(n, dtype=jnp.float32) / n)
    ang = jnp.stack([rows[:, None] * freqs, cols[:, None] * freqs], axis=1)
    cos = jnp.cos(ang)[None, :, None, None]
    sin = jnp.sin(ang)[None, :, None, None]
    xr = x.astype(jnp.float32).reshape(x.shape[:-1] + (2, 2, n))
    x1, x2 = xr[..., 0, :], xr[..., 1, :]
    out = jnp.stack([x1 * cos - x2 * sin, x2 * cos + x1 * sin], axis=-2)
    return out.reshape(x.shape).astype(x.dtype)


def mlstm_scan(q, k, v, ig, lf, state):
    B, H, S, Dh = q.shape
    nc = S // ML_CHUNK
    tril = jnp.tril(jnp.ones((ML_CHUNK, ML_CHUNK), dtype=bool))

    def chunks(t):
        return jnp.moveaxis(t.reshape(t.shape[:2] + (nc, ML_CHUNK) + t.shape[3:]), 2, 0)

    def step(carry, inp):
        C, n, m = carry
        qc, kc, vc, ic, fc = inp
        b = jnp.cumsum(fc, axis=-1)
        dmat = jnp.where(tril, b[..., :, None] - b[..., None, :] + ic[..., None, :], -jnp.inf)
        inter = b + m[..., None]
        m_t = jnp.maximum(inter, jnp.max(dmat, axis=-1))
        wt = jnp.exp(dmat - m_t[..., None])
        a = jnp.exp(inter - m_t)
        s = jnp.einsum('bhtd,bhsd->bhts', qc, kc) * wt
        num = a[..., None] * jnp.einsum('bhvk,bhtk->bhtv', C, qc) + jnp.einsum('bhts,bhsv->bhtv', s, vc)
        den = a * jnp.einsum('bhk,bhtk->bht', n, qc) + jnp.sum(s, axis=-1)
        h = num / jnp.maximum(jnp.abs(den), jnp.exp(-m_t))[..., None]
        b_end = b[..., -1]
        g = b_end[..., None] - b + ic
        m_new = jnp.maximum(b_end + m, jnp.max(g, axis=-1))
        decay = jnp.exp(b_end + m - m_new)
        wk = jnp.exp(g - m_new[..., None])[..., None] * kc
        C_new = decay[..., None, None] * C + jnp.einsum('bhsv,bhsk->bhvk', vc, wk)
        n_new = decay[..., None] * n + jnp.sum(wk, axis=2)
        return (C_new, n_new, m_new), h

    state, h = lax.scan(step, state, (chunks(q), chunks(k), chunks(v), chunks(ig), chunks(lf)))
    h = jnp.moveaxis(h, 0, 2).reshape(B, H, S, Dh)
    return state, h


def mlstm_branch(qk, v, o, g, qk_c, v_c, o_c, g_c, conv_w, conv_b, gate_b, norm_w, with_ctx):
    def prep(qk_s, v_s, g_s):
        B, S, _ = v_s.shape
        qk_s = jax.nn.silu(dwconv(qk_s, conv_w, conv_b))
        q, k = jnp.split(qk_s, 2, axis=-1)

        def heads(t):
            return t.astype(jnp.float32).reshape(B, S, ML_HEADS, ML_DH).transpose(0, 2, 1, 3)

        gg = (g_s.reshape(B, S, 2, 2, ML_HEADS) + gate_b).astype(jnp.float32).transpose(2, 3, 0, 4, 1)
        return heads(q) * ML_DH ** -0.5, heads(k), heads(v_s), gg[:, 0], jax.nn.log_sigmoid(gg[:, 1])

    lat = prep(qk, v, g)
    ctx = prep(qk_c, v_c, g_c)
    B = v.shape[0]
    zero = (jnp.zeros((B, ML_HEADS, ML_DH, ML_DH), jnp.float32),
            jnp.zeros((B, ML_HEADS, ML_DH), jnp.float32),
            jnp.zeros((B, ML_HEADS), jnp.float32))
    hs, hcs = [], []
    for d in range(2):
        fl = (lambda t: jnp.flip(t, axis=2)) if d else (lambda t: t)
        st, hc_d = mlstm_scan(fl(ctx[0]), fl(ctx[1]), fl(ctx[2]), fl(ctx[3][d]), fl(ctx[4][d]), zero)
        _, h_d = mlstm_scan(fl(lat[0]), fl(lat[1]), fl(lat[2]), fl(lat[3][d]), fl(lat[4][d]), st)
        hs.append(fl(h_d))
        hcs.append(fl(hc_d))

    def finish(h, o_s):
        Bn, H, S, Dh = h.shape
        h = rmsnorm(h.transpose(0, 2, 1, 3), norm_w.reshape(H, Dh)).reshape(Bn, S, H * Dh)
        return h.astype(o_s.dtype) * jax.nn.sigmoid(o_s)

    out = finish(hs[0] + hs[1], o)
    out_c = finish(hcs[0] + hcs[1], o_c) if with_ctx else None
    return out, out_c


def diff_attend(q, k, v, lam):
    s = jnp.einsum('bhcqd,bhckd->bhcqk', q, k).astype(jnp.float32) * DA_DK ** -0.5
    p = jax.nn.softmax(s, axis=-1)
    a = p[:, :, 0] - lam * p[:, :, 1]
    return jnp.einsum('bhqk,bhkv->bhqv', a.astype(v.dtype), v)


def diff_branch(q, k, v, qc, kc, vc, lam_p, subln_w, lam_init, rows, cols, with_ctx):
    B, T, _ = q.shape
    Lc = qc.shape[1]

    def qk_heads(t, S):
        return t.reshape(B, S, DA_HEADS, 2, DA_DK)

    def to_bhc(t):
        return t.transpose(0, 2, 3, 1, 4)

    q_l = to_bhc(rope_2d(qk_heads(q, T), rows, cols))
    k_l = to_bhc(rope_2d(qk_heads(k, T), rows, cols))
    q_c = to_bhc(qk_heads(qc, Lc))
    k_c = to_bhc(qk_heads(kc, Lc))
    v_l = v.reshape(B, T, DA_HEADS, DA_DV).transpose(0, 2, 1, 3)
    v_c = vc.reshape(B, Lc, DA_HEADS, DA_DV).transpose(0, 2, 1, 3)
    lp = lam_p.astype(jnp.float32)
    lam = jnp.exp(jnp.sum(lp[0] * lp[1])) - jnp.exp(jnp.sum(lp[2] * lp[3])) + lam_init
    k_all = jnp.concatenate([k_l, k_c], axis=3)
    v_all = jnp.concatenate([v_l, v_c], axis=2)
    nb = T // Q_BLOCK
    q_blocks = jnp.moveaxis(q_l.reshape(B, DA_HEADS, 2, nb, Q_BLOCK, DA_DK), 3, 0)
    o = lax.map(lambda qb: diff_attend(qb, k_all, v_all, lam), q_blocks)
    o = o.transpose(1, 0, 3, 2, 4).reshape(B, T, DA_HEADS, DA_DV)

    def finish(t):
        return (rmsnorm(t, subln_w) * (1.0 - lam_init)).reshape(t.shape[0], t.shape[1], DA_HEADS * DA_DV)

    out = finish(o)
    out_c = finish(diff_attend(q_c, k_c, v_c, lam).transpose(0, 2, 1, 3)) if with_ctx else None
    return out, out_c


def fourier_branch(t):
    B, S, _ = t.shape
    tf = t.astype(jnp.float32).reshape(B, S, FN_GROUPS, FN_GD)
    y = jnp.real(jnp.fft.fft2(tf, axes=(1, 3), norm='ortho'))
    return y.reshape(B, S, FN_GROUPS * FN_GD).astype(t.dtype)


def sgu_branch(u, v, norm_w, w_s, b_s):
    B, S, _ = u.shape
    u = jax.nn.gelu(u)
    v = layernorm(jax.nn.gelu(v), norm_w)
    v = v.reshape(B, S // SG_CHUNK, SG_CHUNK, SG_GROUPS, SG_GD)
    v = jnp.einsum('gpq,bnqgd->bnpgd', w_s, v) + b_s.T[:, :, None]
    return u * v.reshape(B, S, SG_GROUPS * SG_GD)


def merge(branches, gate_pre, w_branch, w_out):
    br = jnp.stack(branches, axis=2)
    y = jnp.einsum('bsgc,gcd->bsgd', br, w_branch)
    gt = jax.nn.sigmoid(gate_pre.reshape(gate_pre.shape[:2] + (N_BRANCH, D_MODEL)))
    return jnp.einsum('bsgd,bsgd->bsd', gt, y) @ w_out


def mixer_sublayer(h, hc, rows, cols, lam_init, with_ctx, w_in, ml_conv_w, ml_conv_b, ml_gate_b, ml_norm,
                   da_lam, da_subln, sg_norm, sg_w, sg_b, w_branch, w_out):
    P = split_cols(h @ w_in)
    Pc = split_cols(hc @ w_in)
    m, mc = mlstm_branch(P[0], P[1], P[2], P[3], Pc[0], Pc[1], Pc[2], Pc[3],
                         ml_conv_w, ml_conv_b, ml_gate_b, ml_norm, with_ctx)
    d, dc = diff_branch(P[4], P[5], P[6], Pc[4], Pc[5], Pc[6], da_lam, da_subln, lam_init, rows, cols, with_ctx)
    f = fourier_branch(P[7])
    s = sgu_branch(P[8], P[9], sg_norm, sg_w, sg_b)
    y = merge([m, d, f, s], P[10], w_branch, w_out)
    yc = None
    if with_ctx:
        fc = fourier_branch(Pc[7])
        sc = sgu_branch(Pc[8], Pc[9], sg_norm, sg_w, sg_b)
        yc = merge([mc, dc, fc, sc], Pc[10], w_branch, w_out)
    return y, yc


def conv_ffn(h, w_up, conv_w, conv_b, w_down):
    a, g = jnp.split(h @ w_up, 2, axis=-1)
    return (jax.nn.silu(dwconv(a, conv_w, conv_b)) * g) @ w_down


def setup_inputs(seed: int = 0) -> dict:
    key = jax.random.key(seed)
    ks = jax.random.split(key, 24)
    L, D, H = DEPTH, D_MODEL, ML_HEADS

    def nrm(k, shape, s):
        return jax.random.normal(k, shape, jnp.float32) * s

    gate_b = jnp.stack([nrm(ks[10], (L, 2, H), 0.1),
                        jnp.linspace(3.0, 6.0, H, dtype=jnp.float32) + nrm(ks[11], (L, 2, H), 0.1)], axis=2)
    return {
        'x': nrm(ks[0], (BATCH, SEQ, D), 1.0),
        'c': nrm(ks[1], (BATCH, D), 1.0),
        'ctx': nrm(ks[2], (BATCH, CTX_LEN, D), 1.0),
        'c_ctx': nrm(ks[3], (D,), 1.0),
        'w_ada': nrm(ks[4], (L, D, 6 * D), 0.5 * D ** -0.5),
        'b_ada': nrm(ks[5], (L, 6 * D), 0.02),
        'norm_g': 1.0 + nrm(ks[6], (L, 4, D), 0.05),
        'w_in': nrm(ks[7], (L, D, IN_WIDTH), D ** -0.5),
        'ml_conv_w': nrm(ks[8], (L, CONV_W, 2 * ML_HEADS * ML_DH), 0.5),
        'ml_conv_b': nrm(ks[9], (L, 2 * ML_HEADS * ML_DH), 0.02),
        'ml_gate_b': gate_b,
        'ml_norm': 1.0 + nrm(ks[12], (L, ML_HEADS * ML_DH), 0.05),
        'da_lam': nrm(ks[13], (L, 4, DA_DK), 0.1),
        'da_subln': 1.0 + nrm(ks[14], (L, DA_DV), 0.05),
        'sg_norm': 1.0 + nrm(ks[15], (L, SG_GROUPS * SG_GD), 0.05),
        'sg_w': nrm(ks[16], (L, SG_GROUPS, SG_CHUNK, SG_CHUNK), SG_CHUNK ** -0.5),
        'sg_b': 1.0 + nrm(ks[17], (L, SG_GROUPS, SG_CHUNK), 0.1),
        'w_branch': nrm(ks[18], (L, N_BRANCH, BRANCH_W, D), BRANCH_W ** -0.5),
        'w_out': nrm(ks[19], (L, D, D), D ** -0.5),
        'ffn_up': nrm(ks[20], (L, D, 2 * D_FF), D ** -0.5),
        'ffn_conv_w': nrm(ks[21], (L, CONV_W, D_FF), 0.5),
        'ffn_conv_b': nrm(ks[22], (L, D_FF), 0.02),
        'ffn_down': nrm(ks[23], (L, D_FF, D), D_FF ** -0.5),
    }


def reference(x, c, ctx, c_ctx, w_ada, b_ada, norm_g, w_in, ml_conv_w, ml_conv_b, ml_gate_b, ml_norm,
              da_lam, da_subln, sg_norm, sg_w, sg_b, w_branch, w_out, ffn_up, ffn_conv_w, ffn_conv_b, ffn_down):
    B, T, D = x.shape
    ROWS = T // GRID_W
    rows = jnp.repeat(jnp.arange(ROWS, dtype=jnp.float32), GRID_W)
    cols = jnp.tile(jnp.arange(GRID_W, dtype=jnp.float32), ROWS)
    xc = ctx
    sc = jax.nn.silu(c)
    scc = jax.nn.silu(c_ctx)
    for l in range(DEPTH):
        last = l == DEPTH - 1
        lam_init = 0.8 - 0.6 * math.exp(-0.3 * l)
        mod = (sc @ w_ada[l] + b_ada[l]).reshape(B, 6, 1, D)
        modc = (scc @ w_ada[l] + b_ada[l]).reshape(6, 1, D)
        h = rmsnorm(x, norm_g[l, 0]) * (1.0 + mod[:, 1]) + mod[:, 0]
        hc = rmsnorm(xc, norm_g[l, 0]) * (1.0 + modc[1]) + modc[0]
        y, yc = mixer_sublayer(h, hc, rows, cols, lam_init, not last, w_in[l], ml_conv_w[l], ml_conv_b[l],
                               ml_gate_b[l], ml_norm[l], da_lam[l], da_subln[l], sg_norm[l], sg_w[l], sg_b[l],
                               w_branch[l], w_out[l])
        x = x + mod[:, 2] * rmsnorm(y, norm_g[l, 1])
        h = rmsnorm(x, norm_g[l, 2]) * (1.0 + mod[:, 4]) + mod[:, 3]
        x = x + mod[:, 5] * rmsnorm(conv_ffn(h, ffn_up[l], ffn_conv_w[l], ffn_conv_b[l], ffn_down[l]), norm_g[l, 3])
        if not last:
            xc = xc + modc[2] * rmsnorm(yc, norm_g[l, 1])
            hc = rmsnorm(xc, norm_g[l, 2]) * (1.0 + modc[4]) + modc[3]
            xc = xc + modc[5] * rmsnorm(conv_ffn(hc, ffn_up[l], ffn_conv_w[l], ffn_conv_b[l], ffn_down[l]), norm_g[l, 3])
    return x
```

```python
import numpy as np
from contextlib import ExitStack, contextmanager
import concourse.bass as bass
import concourse.mybir as mybir
from concourse.bass_utils import run_bass_kernel_spmd

F32 = mybir.dt.float32
BF16 = mybir.dt.bfloat16
AF = mybir.ActivationFunctionType
ALU = mybir.AluOpType
AX = mybir.AxisListType

ENGS = ("pe", "act", "dve", "pool", "sp")


class T:
    def __init__(self, name, ap):
        self.name = name
        self.ap = ap
        self.w = None
        self.r = []

    def __getitem__(self, idx):
        return self.ap[idx]


class K:
    def __init__(self, nc, n_dma_sems=24):
        self.nc = nc
        self.es = ExitStack()
        self.eng = {"pe": nc.tensor, "act": nc.scalar, "dve": nc.vector, "pool": nc.gpsimd, "sp": nc.sync}
        self.sem = {e: self.es.enter_context(nc.semaphore("s_" + e)) for e in ENGS}
        self.cnt = {e: 0 for e in ENGS}
        self.waited = {e: {x: 0 for x in ENGS} for e in ENGS}
        self.ops = {e: [] for e in ENGS}
        self.dsem = [self.es.enter_context(nc.semaphore("d%d" % i)) for i in range(n_dma_sems)]
        self.dcnt = [0] * n_dma_sems
        self.dnext = 0
        self.dwaited = {e: [0] * n_dma_sems for e in ENGS}
        self.n_inst = 0

    pending = None
    scope_tiles = None

    def sb(self, name, shape, dtype=F32):
        t = self.es.enter_context(self.nc.sbuf_tensor(name, list(shape), dtype))
        tt = T(name, t)
        if self.pending:
            tt.r = list(self.pending)
        if self.scope_tiles is not None:
            self.scope_tiles.append(tt)
        return tt

    @contextmanager
    def scope(self):
        outer = self.es
        inner = ExitStack()
        self.es = inner
        self.scope_tiles = []
        try:
            yield
        finally:
            if self.pending is None:
                self.pending = []
            for t in self.scope_tiles:
                if t.w is not None:
                    self.pending.append(t.w)
                self.pending.extend(t.r)
            best = {}
            for d in self.pending:
                key = (d[0], d[1])
                if key not in best or best[key][2] < d[2]:
                    best[key] = d
            self.pending = list(best.values())
            self.scope_tiles = None
            inner.close()
            self.es = outer

    def ps(self, name, shape, dtype=F32):
        t = self.es.enter_context(self.nc.psum_tensor(name, list(shape), dtype))
        return T(name, t)

    def dram(self, name, shape, dtype, kind):
        t = self.nc.dram_tensor(name, list(shape), dtype, kind=kind)
        return T(name, t.ap())

    def _deps(self, reads, writes):
        deps = []
        for t in reads:
            if t.w is not None:
                deps.append(t.w)
        for t in writes:
            if t.w is not None:
                deps.append(t.w)
            deps.extend(t.r)
        return deps

    def _waits(self, e, deps):
        ws = []
        for d in deps:
            if d[0] == "c":
                _, x, c = d
                if x == e and (e == "pe" or not self.same_engine_sync):
                    continue
                if self.waited[e][x] >= c:
                    continue
                self.waited[e][x] = c
                ws.append((self.sem[x], c))
            else:
                _, i, v = d
                if self.dwaited[e][i] >= v:
                    continue
                self.dwaited[e][i] = v
                ws.append((self.dsem[i], v))
        best = {}
        for s, v in ws:
            k = id(s)
            if k not in best or best[k][1] < v:
                best[k] = (s, v)
        return list(best.values())

    same_engine_sync = True

    def op(self, e, fn, reads=(), writes=()):
        deps = self._deps(reads, writes)
        ws = self._waits(e, deps)
        self.cnt[e] += 1
        tok = ("c", e, self.cnt[e])
        eng = self.eng[e]
        for s_, v_ in ws:
            eng.wait_ge(s_, v_)
        fn(eng).then_inc(self.sem[e], 1)
        for t in reads:
            t.r.append(tok)
        for t in writes:
            t.w = tok
            t.r = []
        self.n_inst += 1
        return tok

    def dma(self, q, out_ap, in_ap, reads=(), writes=()):
        deps = self._deps(reads, writes)
        i = self.dnext
        self.dnext = (self.dnext + 1) % len(self.dsem)
        if self.dcnt[i] > 0:
            deps.append(("d", i, self.dcnt[i]))
        ws = self._waits(q, deps)
        self.dcnt[i] += 16
        tok = ("d", i, self.dcnt[i])
        eng = self.eng[q]
        for s_, v_ in ws:
            eng.wait_ge(s_, v_)
        eng.dma_start(out=out_ap, in_=in_ap).then_inc(self.dsem[i], 16)
        for t in reads:
            t.r.append(tok)
        for t in writes:
            t.w = tok
            t.r = []
        self.n_inst += 1
        return tok

    def finish(self, final_tiles):
        deps = []
        for t in final_tiles:
            if t.w is not None:
                deps.append(t.w)
        ws = self._waits("sp", deps)
        for s_, v_ in ws:
            self.eng["sp"].wait_ge(s_, v_)
        self.es.close()

import math

NT = 2315
LAT_END = 2058
CTS = [(0, 512), (512, 1024), (1024, 1536), (1536, 2048), (2048, 2315)]
EPS = 1e-6
GC1 = 0.044715
GC2 = 1.5957691216057308


def lam(f, *a):
    return lambda e: f(e, *a)


class PsumRot:
    def __init__(self, k, n, shape=(128, 512), dtype=F32, prefix="ps"):
        self.tiles = [k.ps("%s%d" % (prefix, i), shape, dtype) for i in range(n)]
        self.i = 0

    def get(self):
        t = self.tiles[self.i]
        self.i = (self.i + 1) % len(self.tiles)
        return t


class SbRot:
    def __init__(self, k, n, shape, dtype, prefix):
        self.tiles = [k.sb("%s%d" % (prefix, i), shape, dtype) for i in range(n)]
        self.i = 0

    def get(self):
        t = self.tiles[self.i]
        self.i = (self.i + 1) % len(self.tiles)
        return t


def load_w_block(k, wdram, col0, ncols, wf, wb, nk=8, q="sp", cast_eng="pool"):
    src = wdram.ap[:, col0:col0 + ncols].rearrange("(kc p) c -> p kc c", p=128)
    k.dma(q, wf.ap[:, 0:nk, 0:ncols], src, reads=[wdram], writes=[wf])
    k.op(cast_eng, lambda e: e.tensor_copy(out=wb.ap[:, 0:nk, 0:ncols], in_=wf.ap[:, 0:nk, 0:ncols]), reads=[wf], writes=[wb])


def stream_rstd(k, xT, nchunk, rstd, psrot, xrot, sqrot, inv_n, cts=None):
    for (c0, c1) in (cts or CTS):
        n = c1 - c0
        ps = psrot.get()
        for kc in range(nchunk):
            xt = xrot.get(); sq = sqrot.get()
            k.dma("sp", xt.ap[:, 0:n], xT.ap[kc * 128:(kc + 1) * 128, c0:c1], reads=[xT], writes=[xt])
            if kc % 2 == 0:
                k.op("act", lambda e, sq=sq, xt=xt: e.activation(out=sq.ap[:, 0:n], in_=xt.ap[:, 0:n], func=AF.Square), reads=[xt], writes=[sq])
            else:
                k.op("pool", lambda e, sq=sq, xt=xt: e.tensor_tensor(out=sq.ap[:, 0:n], in0=xt.ap[:, 0:n], in1=xt.ap[:, 0:n], op=ALU.mult), reads=[xt], writes=[sq])
            k.op("pe", lambda e, sq=sq, kc=kc, ps=ps: e.matmul(ps.ap[:, 0:n], lhsT=k.ones_f.ap[:, :], rhs=sq.ap[:, 0:n], start=(kc == 0), stop=(kc == nchunk - 1)), reads=[k.ones_f, sq], writes=[ps])
        k.op("act", lambda e, ps=ps: e.activation(out=rstd.ap[:, c0:c1], in_=ps.ap[:, 0:n], func=AF.Sqrt, bias=k.eps_ap, scale=inv_n), reads=[ps], writes=[rstd])
        k.op("dve", lambda e: e.reciprocal(out=rstd.ap[:, c0:c1], in_=rstd.ap[:, c0:c1]), reads=[rstd], writes=[rstd])


def seg_split(c0, c1):
    out = []
    if c0 < LAT_END:
        out.append((0, c0, min(c1, LAT_END)))
    if c1 > LAT_END:
        out.append((1, max(c0, LAT_END), c1))
    return out


def stream_modnorm(k, xT, rstd, A, shift_tile, shift_base, h, xrot, tmprot):
    for (c0, c1) in CTS:
        n = c1 - c0
        for kc in range(8):
            xt = xrot.get(); tmp = tmprot.get()
            k.dma("sp", xt.ap[:, 0:n], xT.ap[kc * 128:(kc + 1) * 128, c0:c1], reads=[xT], writes=[xt])
            for (si, a, b) in seg_split(c0, c1):
                k.op("dve", lambda e, kc=kc, tmp=tmp, xt=xt, si=si, a=a, b=b: e.scalar_tensor_tensor(out=tmp.ap[:, a - c0:b - c0], in0=xt.ap[:, a - c0:b - c0], scalar=A.ap[:, kc, si:si + 1], in1=rstd.ap[:, a:b], op0=ALU.mult, op1=ALU.mult), reads=[xt, A, rstd], writes=[tmp])
                k.op("act", lambda e, kc=kc, tmp=tmp, si=si, a=a, b=b: e.activation(out=h.ap[:, kc, a:b], in_=tmp.ap[:, a - c0:b - c0], func=AF.Identity, bias=shift_tile.ap[:, shift_base + kc, si:si + 1], scale=1.0), reads=[tmp, shift_tile], writes=[h])


def consts(k):
    k.ones_f = k.sb("ones_f", [128, 128], F32)
    k.op("pool", lambda e: e.memset(k.ones_f.ap[:, :], 1.0), writes=[k.ones_f])
    k.eps_t = k.sb("eps_t", [128, 1], F32)
    k.op("pool", lambda e: e.memset(k.eps_t.ap[:, :], EPS), writes=[k.eps_t])
    k.eps_ap = k.eps_t.ap[:, 0:1]


def compute_mod(k, cT, w_ada, b_adaT, modT, psrot, wfrot):
    sc = k.sb("sc", [128, 8, 2], F32)
    k.dma("sp", sc.ap[:, :, :], cT.ap.rearrange("(kc p) j -> p kc j", p=128), reads=[cT], writes=[sc])
    k.op("act", lambda e: e.activation(out=sc.ap[:, :, :], in_=sc.ap[:, :, :], func=AF.Silu), reads=[sc], writes=[sc])
    bt = k.sb("bada", [128, 48], F32)
    k.dma("sp", bt.ap[:, :], b_adaT.ap[:, :], reads=[b_adaT], writes=[bt])
    for fb in range(12):
        ps = psrot.get()
        for j in range(4):
            f = fb * 4 + j
            wf = wfrot.get()
            k.dma("sp", wf.ap[:, :, :], w_ada.ap[:, f * 128:(f + 1) * 128].rearrange("(kc p) c -> p kc c", p=128), reads=[w_ada], writes=[wf])
            for kc in range(8):
                k.op("pe", lambda e, wf=wf, j=j, kc=kc, ps=ps: e.matmul(ps.ap[:, 2 * j:2 * j + 2], lhsT=wf.ap[:, kc, :], rhs=sc.ap[:, kc, :], start=(kc == 0), stop=(kc == 7)), reads=[wf, sc], writes=[ps])
        for j in range(4):
            f = fb * 4 + j
            k.op("dve", lambda e, ps=ps, j=j, f=f: e.tensor_scalar(out=modT.ap[:, f, :], in0=ps.ap[:, 2 * j:2 * j + 2], scalar1=bt.ap[:, f:f + 1], scalar2=None, op0=ALU.add), reads=[ps, bt], writes=[modT])


def gelu_from_psum(k, ps, n, out_ap, out_tile, tmp1, tmp2):
    k.op("act", lambda e: e.activation(out=tmp1.ap[:, 0:n], in_=ps.ap[:, 0:n], func=AF.Square), reads=[ps], writes=[tmp1])
    k.op("dve", lambda e: e.tensor_scalar(out=tmp1.ap[:, 0:n], in0=tmp1.ap[:, 0:n], scalar1=GC1, scalar2=1.0, op0=ALU.mult, op1=ALU.add), reads=[tmp1], writes=[tmp1])
    k.op("dve", lambda e: e.tensor_tensor(out=tmp1.ap[:, 0:n], in0=ps.ap[:, 0:n], in1=tmp1.ap[:, 0:n], op=ALU.mult), reads=[ps, tmp1], writes=[tmp1])
    k.op("act", lambda e: e.activation(out=tmp2.ap[:, 0:n], in_=tmp1.ap[:, 0:n], func=AF.Sigmoid, scale=GC2), reads=[tmp1], writes=[tmp2])
    k.op("dve", lambda e: e.tensor_tensor(out=out_ap, in0=ps.ap[:, 0:n], in1=tmp2.ap[:, 0:n], op=ALU.mult), reads=[ps, tmp2], writes=[out_tile])


def build_phase_a():
    nc = bass.Bass("TRN2", target_bir_lowering=False)
    k = K(nc)
    D = k.dram
    xT = D("xT", [1024, NT], F32, "ExternalInput")
    cT = D("cT", [1024, 2], F32, "ExternalInput")
    w_ada = D("w_ada", [1024, 6144], F32, "ExternalInput")
    b_adaT = D("b_adaT", [128, 48], F32, "ExternalInput")
    ng0 = D("ng0", [128, 8], F32, "ExternalInput")
    wA = D("wA", [1024, 25 * 128], F32, "ExternalInput")
    cmask = D("cmask", [1, NT], F32, "ExternalInput")
    ropeC = D("ropeC", [128, NT], F32, "ExternalInput")
    ropeS = D("ropeS", [128, NT], F32, "ExternalInput")
    mlcw = D("mlcw", [128, 4 * 3], F32, "ExternalInput")
    mlcb = D("mlcb", [128, 4], F32, "ExternalInput")
    qksc = D("qksc", [128, 1], F32, "ExternalInput")
    gateb = D("gateb", [16, 2], F32, "ExternalInput")
    sgn = D("sgn", [128, 2], F32, "ExternalInput")
    PT = D("PT", [20 * 128, NT], BF16, "ExternalOutput")
    GT = D("GT", [16, NT], F32, "ExternalOutput")
    modO = D("modO", [128, 96], F32, "ExternalOutput")
    hT = D("hT", [1024, NT], BF16, "ExternalOutput")

    consts(k)
    psrot = PsumRot(k, 6)
    sqrot = SbRot(k, 3, [128, 512], F32, "sq")
    xrot = SbRot(k, 4, [128, 512], F32, "xr")
    wfrot = SbRot(k, 4, [128, 8, 128], F32, "wf")
    wbrot = SbRot(k, 4, [128, 8, 128], BF16, "wb")
    modT = k.sb("modT", [128, 48, 2], F32)
    compute_mod(k, cT, w_ada, b_adaT, modT, psrot, wfrot)
    k.dma("sp", modO.ap[:, :], modT.ap[:, :, :].rearrange("p a b -> p (a b)"), reads=[modT], writes=[modO])
    g0 = k.sb("g0", [128, 8], F32)
    k.dma("sp", g0.ap[:, :], ng0.ap[:, :], reads=[ng0], writes=[g0])
    A1 = k.sb("A1", [128, 8, 2], F32)
    for kc in range(8):
        k.op("dve", lambda e, kc=kc: e.tensor_scalar(out=A1.ap[:, kc, :], in0=modT.ap[:, 8 + kc, :], scalar1=1.0, scalar2=g0.ap[:, kc:kc + 1], op0=ALU.add, op1=ALU.mult), reads=[modT, g0], writes=[A1])
    mask = k.sb("mask", [128, NT], F32)
    k.dma("sp", mask.ap[:, :], cmask.ap[0:1, :].partition_broadcast(128), reads=[cmask], writes=[mask])
    rc = k.sb("rc", [128, NT], F32); rs = k.sb("rs", [128, NT], F32)
    k.dma("sp", rc.ap[:, :], ropeC.ap[:, :], reads=[ropeC], writes=[rc])
    k.dma("sp", rs.ap[:, :], ropeS.ap[:, :], reads=[ropeS], writes=[rs])
    cw = k.sb("cw", [128, 12], F32); cb = k.sb("cb", [128, 4], F32); qs = k.sb("qs", [128, 1], F32)
    k.dma("sp", cw.ap[:, :], mlcw.ap[:, :], reads=[mlcw], writes=[cw])
    k.dma("sp", cb.ap[:, :], mlcb.ap[:, :], reads=[mlcb], writes=[cb])
    k.dma("sp", qs.ap[:, :], qksc.ap[:, :], reads=[qksc], writes=[qs])
    gb = k.sb("gb", [16, 2], F32)
    k.dma("sp", gb.ap[:, :], gateb.ap[:, :], reads=[gateb], writes=[gb])
    sgw = k.sb("sgw", [128, 2], F32)
    k.dma("sp", sgw.ap[:, :], sgn.ap[:, :], reads=[sgn], writes=[sgw])
    rstd = k.sb("rstd", [128, NT], F32)
    stream_rstd(k, xT, 8, rstd, psrot, xrot, sqrot, 1.0 / 1024)
    h = k.sb("h", [128, 8, NT], BF16)
    tmprot = SbRot(k, 2, [128, 512], F32, "tmpn")
    stream_modnorm(k, xT, rstd, A1, modT, 0, h, xrot, tmprot)
    for kc in range(8):
        k.dma("sp", hT.ap[kc * 128:(kc + 1) * 128, :], h.ap[:, kc, :], reads=[h], writes=[hT])

    NB = 25
    wtiles = {}

    def prefetch(b):
        if b < NB and b not in wtiles:
            wf = wfrot.get(); wb = wbrot.get()
            load_w_block(k, wA, b * 128, 128, wf, wb)
            wtiles[b] = wb

    def mm(wb, ct, ps, M=128):
        c0, c1 = CTS[ct]
        for kc in range(8):
            k.op("pe", lambda e, kc=kc: e.matmul(ps.ap[0:M, 0:c1 - c0], lhsT=wb.ap[:, kc, 0:M], rhs=h.ap[:, kc, c0:c1], start=(kc == 0), stop=(kc == 7)), reads=[wb, h], writes=[ps])

    orot = SbRot(k, 4, [128, 512], BF16, "ost")
    t1rot = SbRot(k, 2, [128, 512], F32, "gt1")
    t2rot = SbRot(k, 2, [128, 512], F32, "gt2")
    rawrot = SbRot(k, 1, [128, NT], F32, "raw")
    crot = SbRot(k, 1, [128, NT], F32, "cnv")
    obig = SbRot(k, 2, [128, NT], BF16, "obig")
    prefetch(0); prefetch(1)
    order = [0, 1, 2, 3, 4, 5, 6, 7, 8, 12, 9, 13, 10, 14, 11, 15, 16, 17, 18, 19, 20, 21, 22, 23, 24]
    pos = {b: i for i, b in enumerate(order)}
    wtiles.clear(); wfrot.i = 0; wbrot.i = 0
    k_ops_before = None

    def pf(i):
        if i < len(order):
            prefetch(order[i])
    pf(0); pf(1)
    sgv_f = k.sb("sgv_f", [128, 2, NT], F32)
    i = 0
    while i < len(order):
        b = order[i]
        pf(i + 2)
        wb = wtiles[b]
        if b < 4:
            raw = rawrot.get(); cv = crot.get(); ob = obig.get()
            for ct, (c0, c1) in enumerate(CTS):
                ps = psrot.get(); mm(wb, ct, ps)
                k.op("dve", lambda e, ps=ps, c0=c0, c1=c1, raw=raw: e.tensor_tensor(out=raw.ap[:, c0:c1], in0=ps.ap[:, 0:c1 - c0], in1=mask.ap[:, c0:c1], op=ALU.mult), reads=[ps, mask], writes=[raw])
            n = NT - 2
            k.op("dve", lambda e, raw=raw, cv=cv, b=b: e.tensor_scalar(out=cv.ap[:, 1:1 + n], in0=raw.ap[:, 1:1 + n], scalar1=cw.ap[:, b * 3 + 1:b * 3 + 2], scalar2=cb.ap[:, b:b + 1], op0=ALU.mult, op1=ALU.add), reads=[raw, cw, cb], writes=[cv])
            k.op("dve", lambda e, raw=raw, cv=cv, b=b: e.scalar_tensor_tensor(out=cv.ap[:, 1:1 + n], in0=raw.ap[:, 0:n], scalar=cw.ap[:, b * 3:b * 3 + 1], in1=cv.ap[:, 1:1 + n], op0=ALU.mult, op1=ALU.add), reads=[raw, cw, cv], writes=[cv])
            k.op("dve", lambda e, raw=raw, cv=cv, b=b: e.scalar_tensor_tensor(out=cv.ap[:, 1:1 + n], in0=raw.ap[:, 2:2 + n], scalar=cw.ap[:, b * 3 + 2:b * 3 + 3], in1=cv.ap[:, 1:1 + n], op0=ALU.mult, op1=ALU.add), reads=[raw, cw, cv], writes=[cv])
            k.op("act", lambda e, cv=cv: e.activation(out=cv.ap[:, 1:1 + n], in_=cv.ap[:, 1:1 + n], func=AF.Silu), reads=[cv], writes=[cv])
            k.op("pool", lambda e, ob=ob: e.memset(ob.ap[:, :], 0.0), writes=[ob])
            k.op("pool", lambda e, cv=cv, ob=ob: e.tensor_scalar(out=ob.ap[:, 1:1 + n], in0=cv.ap[:, 1:1 + n], scalar1=qs.ap[:, 0:1], scalar2=None, op0=ALU.mult), reads=[cv, qs], writes=[ob])
            k.dma("sp", PT.ap[b * 128:(b + 1) * 128, :], ob.ap[:, :], reads=[ob], writes=[PT])
            i += 1
        elif 8 <= b < 12:
            b2 = order[i + 1]
            pf(i + 3)
            wb2 = wtiles[b2]
            for ct, (c0, c1) in enumerate(CTS):
                n = c1 - c0
                ps = psrot.get(); mm(wb, ct, ps)
                ps2 = psrot.get(); mm(wb2, ct, ps2)
                t1 = t1rot.get(); t2 = t2rot.get(); o = orot.get()
                k.op("dve", lambda e, ps=ps, t1=t1, c0=c0, c1=c1, n=n: e.tensor_tensor(out=t1.ap[:, 0:n], in0=ps.ap[:, 0:n], in1=rc.ap[:, c0:c1], op=ALU.mult), reads=[ps, rc], writes=[t1])
                k.op("dve", lambda e, ps2=ps2, t2=t2, c0=c0, c1=c1, n=n: e.tensor_tensor(out=t2.ap[:, 0:n], in0=ps2.ap[:, 0:n], in1=rs.ap[:, c0:c1], op=ALU.mult), reads=[ps2, rs], writes=[t2])
                k.op("pool", lambda e, t1=t1, t2=t2, o=o, n=n: e.tensor_tensor(out=o.ap[:, 0:n], in0=t1.ap[:, 0:n], in1=t2.ap[:, 0:n], op=ALU.add), reads=[t1, t2], writes=[o])
                k.dma("sp", PT.ap[b * 128:(b + 1) * 128, c0:c1], o.ap[:, 0:n], reads=[o], writes=[PT])
            i += 2
        elif b in (4, 5, 6, 7, 16, 17, 18, 19):
            ob_idx = b if b < 8 else b - 4
            for ct, (c0, c1) in enumerate(CTS):
                n = c1 - c0
                ps = psrot.get(); mm(wb, ct, ps)
                o = orot.get()
                k.op("act", lambda e, ps=ps, o=o, n=n: e.activation(out=o.ap[:, 0:n], in_=ps.ap[:, 0:n], func=AF.Copy), reads=[ps], writes=[o])
                k.dma("sp", PT.ap[ob_idx * 128:(ob_idx + 1) * 128, c0:c1], o.ap[:, 0:n], reads=[o], writes=[PT])
            i += 1
        elif b in (20, 21):
            ob_idx = b - 4
            for ct, (c0, c1) in enumerate(CTS):
                n = c1 - c0
                ps = psrot.get(); mm(wb, ct, ps)
                o = orot.get(); t1 = t1rot.get(); t2 = t2rot.get()
                gelu_from_psum(k, ps, n, o.ap[:, 0:n], o, t1, t2)
                k.dma("sp", PT.ap[ob_idx * 128:(ob_idx + 1) * 128, c0:c1], o.ap[:, 0:n], reads=[o], writes=[PT])
            i += 1
        elif b in (22, 23):
            for ct, (c0, c1) in enumerate(CTS):
                n = c1 - c0
                ps = psrot.get(); mm(wb, ct, ps)
                t1 = t1rot.get(); t2 = t2rot.get()
                gelu_from_psum(k, ps, n, sgv_f.ap[:, b - 22, c0:c1], sgv_f, t1, t2)
            i += 1
            if b == 23:
                for ct, (c0, c1) in enumerate(CTS):
                    n = c1 - c0
                    psm = psrot.get(); pss = psrot.get()
                    for kc in range(2):
                        k.op("pe", lambda e, kc=kc, psm=psm, c0=c0, c1=c1, n=n: e.matmul(psm.ap[:, 0:n], lhsT=k.ones_f.ap[:, :], rhs=sgv_f.ap[:, kc, c0:c1], start=(kc == 0), stop=(kc == 1)), reads=[k.ones_f, sgv_f], writes=[psm])
                    for kc in range(2):
                        sq = sqrot.get()
                        k.op("pool", lambda e, kc=kc, sq=sq, c0=c0, c1=c1, n=n: e.tensor_tensor(out=sq.ap[:, 0:n], in0=sgv_f.ap[:, kc, c0:c1], in1=sgv_f.ap[:, kc, c0:c1], op=ALU.mult), reads=[sgv_f], writes=[sq])
                        k.op("pe", lambda e, kc=kc, sq=sq, pss=pss, n=n: e.matmul(pss.ap[:, 0:n], lhsT=k.ones_f.ap[:, :], rhs=sq.ap[:, 0:n], start=(kc == 0), stop=(kc == 1)), reads=[k.ones_f, sq], writes=[pss])
                    mean = t1rot.get(); var = t2rot.get()
                    k.op("act", lambda e, psm=psm, mean=mean, n=n: e.activation(out=mean.ap[:, 0:n], in_=psm.ap[:, 0:n], func=AF.Copy, scale=1.0 / 256), reads=[psm], writes=[mean])
                    msq = sqrot.get()
                    k.op("pool", lambda e, mean=mean, msq=msq, n=n: e.tensor_tensor(out=msq.ap[:, 0:n], in0=mean.ap[:, 0:n], in1=mean.ap[:, 0:n], op=ALU.mult), reads=[mean], writes=[msq])
                    k.op("dve", lambda e, pss=pss, msq=msq, var=var, n=n: e.scalar_tensor_tensor(out=var.ap[:, 0:n], in0=pss.ap[:, 0:n], scalar=1.0 / 256, in1=msq.ap[:, 0:n], op0=ALU.mult, op1=ALU.subtract), reads=[pss, msq], writes=[var])
                    k.op("act", lambda e, var=var, n=n: e.activation(out=var.ap[:, 0:n], in_=var.ap[:, 0:n], func=AF.Sqrt, bias=k.eps_ap, scale=1.0), reads=[var], writes=[var])
                    k.op("dve", lambda e, var=var, n=n: e.reciprocal(out=var.ap[:, 0:n], in_=var.ap[:, 0:n]), reads=[var], writes=[var])
                    for kc in range(2):
                        o = orot.get(); sq = sqrot.get()
                        k.op("dve", lambda e, kc=kc, sq=sq, mean=mean, c0=c0, c1=c1, n=n: e.tensor_tensor(out=sq.ap[:, 0:n], in0=sgv_f.ap[:, kc, c0:c1], in1=mean.ap[:, 0:n], op=ALU.subtract), reads=[sgv_f, mean], writes=[sq])
                        k.op("dve", lambda e, kc=kc, sq=sq, var=var, o=o, n=n: e.scalar_tensor_tensor(out=o.ap[:, 0:n], in0=sq.ap[:, 0:n], scalar=sgw.ap[:, kc:kc + 1], in1=var.ap[:, 0:n], op0=ALU.mult, op1=ALU.mult), reads=[sq, sgw, var], writes=[o])
                        k.dma("sp", PT.ap[(18 + kc) * 128:(19 + kc) * 128, c0:c1], o.ap[:, 0:n], reads=[o], writes=[PT])
        else:
            gt_t = rawrot.get(); ls_t = crot.get()
            gt = T("gtv", gt_t.ap[0:16, :]); ls = T("lsv", ls_t.ap[0:16, :])
            gt.w, gt.r, ls.w, ls.r = gt_t.w, gt_t.r, ls_t.w, ls_t.r
            for ct, (c0, c1) in enumerate(CTS):
                n = c1 - c0
                ps = psrot.get(); mm(wb, ct, ps, M=16)
                k.op("act", lambda e, ps=ps, c0=c0, c1=c1, n=n: e.activation(out=gt.ap[:, c0:c1], in_=ps.ap[0:16, 0:n], func=AF.Identity, bias=gb.ap[:, 0:1], scale=1.0), reads=[ps, gb], writes=[gt])
            k.op("act", lambda e: e.activation(out=ls.ap[:, :], in_=gt.ap[:, :], func=AF.Exp, scale=-1.0), reads=[gt], writes=[ls])
            k.op("act", lambda e: e.activation(out=ls.ap[:, :], in_=ls.ap[:, :], func=AF.Ln, bias=1.0, scale=1.0), reads=[ls], writes=[ls])
            k.op("dve", lambda e: e.scalar_tensor_tensor(out=ls.ap[:, :], in0=ls.ap[:, :], scalar=-1.0, in1=gt.ap[:, :], op0=ALU.mult, op1=ALU.subtract), reads=[ls, gt], writes=[ls])
            k.op("dve", lambda e: e.scalar_tensor_tensor(out=gt.ap[:, :], in0=ls.ap[:, :], scalar=gb.ap[:, 1:2], in1=gt.ap[:, :], op0=ALU.mult, op1=ALU.add), reads=[ls, gb, gt], writes=[gt])
            k.dma("sp", GT.ap[:, :], gt.ap[:, :], reads=[gt], writes=[GT])
            i += 1
    k.finish([PT, GT, modO, hT])
    return nc


TT = 8448
NCH = 66


def emit_mlstm(k, psrot, big_bf, ml_qT, ml_kT, ml_ktok, ml_vtok, ml_g, ml_const, o_m, TT=8448, NCH=66, debug=False, dbg=None):
    mc = k.sb("mc", [128, 384], F32)
    k.dma("sp", mc.ap[:, :], ml_const.ap[:, :], reads=[ml_const], writes=[mc])
    ident = mc.ap[:, 0:128]; bigm = mc.ap[:, 128:256]; onesf = mc.ap[:, 256:384]
    qTt = k.sb("mqT", [64, TT], BF16); kTt = k.sb("mkT", [64, TT], BF16)
    vaug2 = k.sb("vaug2", [128, NCH, 128], BF16)
    hbuf = k.sb("hbuf", [64, TT], F32)
    ST = k.sb("ST", [64, 128], F32); STb = k.sb("STb", [128, 128], BF16)
    mp = [k.sb("mp%d" % i, [128, 1], F32) for i in range(2)]
    r128 = SbRot(k, 16, [128, 128], F32, "r128")
    rb128 = SbRot(k, 6, [128, 128], BF16, "rb128")
    gI = SbRot(k, 2, [128, 512], F32, "gI"); gF = SbRot(k, 2, [128, 512], F32, "gF")
    cols1 = SbRot(k, 6, [128, 1], F32, "cols1")
    k.op("pool", lambda e: e.memset(vaug2.ap[:, :, 64:128], 1.0), writes=[vaug2])
    for t_ in rb128.tiles:
        k.op("pool", lambda e, t_=t_: e.memset(t_.ap[:, :], 0.0), writes=[t_])
    for d in range(2):
        k.dma("sp", qTt.ap[:, :], ml_qT.ap[d, :, :], reads=[ml_qT], writes=[qTt])
        k.dma("sp", kTt.ap[:, :], ml_kT.ap[d, :, :], reads=[ml_kT], writes=[kTt])
        k.dma("sp", big_bf.ap[:, :], ml_vtok.ap[d, :, :], reads=[ml_vtok], writes=[big_bf])
        k.op("pool", lambda e: e.tensor_copy(out=vaug2.ap[:, :, 0:64], in_=big_bf.ap[:, :].rearrange("p (n d) -> p n d", d=64)), reads=[big_bf], writes=[vaug2])
        k.dma("sp", big_bf.ap[:, :], ml_ktok.ap[d, :, :], reads=[ml_ktok], writes=[big_bf])
        k.op("pool", lambda e: e.memset(ST.ap[:, :], 0.0), writes=[ST])
        k.op("pool", lambda e: e.memset(STb.ap[:, :], 0.0), writes=[STb])
        k.op("pool", lambda e: e.memset(mp[0].ap[:, :], 0.0), writes=[mp[0]])
        Ig = Fg = None
        for n in range(NCH):
            if n % 4 == 0:
                w = min(512, TT - n * 128)
                Ig = gI.get(); Fg = gF.get()
                k.dma("sp", Ig.ap[:, 0:w], ml_g.ap[d, 0:1, n * 128:n * 128 + w].partition_broadcast(128), reads=[ml_g], writes=[Ig])
                k.dma("sp", Fg.ap[:, 0:w], ml_g.ap[d, 1:2, n * 128:n * 128 + w].partition_broadcast(128), reads=[ml_g], writes=[Fg])
            o = (n % 4) * 128
            cs = slice(n * 128, (n + 1) * 128)
            mprev = mp[n % 2]; mnew = mp[(n + 1) % 2]
            Bc = r128.get(); U = r128.get(); M = r128.get(); junk = r128.get(); Z = r128.get(); Wt = r128.get()
            k.op("dve", lambda e: e.tensor_tensor_scan(out=Bc.ap[:, :], data0=onesf, data1=Fg.ap[:, o:o + 128], initial=0.0, op0=ALU.mult, op1=ALU.add), reads=[mc, Fg], writes=[Bc])
            k.op("dve", lambda e: e.tensor_tensor(out=U.ap[:, :], in0=Ig.ap[:, o:o + 128], in1=Bc.ap[:, :], op=ALU.subtract), reads=[Ig, Bc], writes=[U])
            k.op("dve", lambda e: e.tensor_tensor_scan(out=M.ap[:, :], data0=U.ap[:, :], data1=U.ap[:, :], initial=mprev.ap[:, 0:1], op0=ALU.max, op1=ALU.max), reads=[U, mprev], writes=[M])
            k.op("dve", lambda e: e.tensor_tensor(out=mnew.ap[:, :], in0=Bc.ap[:, 127:128], in1=M.ap[:, 127:128], op=ALU.add), reads=[Bc, M], writes=[mnew])
            ucol = cols1.get(); ecol = cols1.get()
            k.op("dve", lambda e: e.tensor_tensor(out=junk.ap[:, :], in0=U.ap[:, :], in1=ident, op=ALU.mult), reads=[U, mc], writes=[junk])
            k.op("dve", lambda e: e.tensor_reduce(out=ucol.ap[:, 0:1], in_=junk.ap[:, :], axis=AX.X, op=ALU.add), reads=[junk], writes=[ucol])
            k.op("pool", lambda e: e.tensor_tensor(out=Z.ap[:, :], in0=M.ap[:, :], in1=bigm, op=ALU.add), reads=[M, mc], writes=[Z])
            k.op("act", lambda e: e.activation(out=Wt.ap[:, :], in_=Z.ap[:, :], func=AF.Exp, bias=ucol.ap[:, 0:1], scale=-1.0), reads=[Z, ucol], writes=[Wt])
            pS = psrot.get()
            k.op("pe", lambda e: e.matmul(pS.ap[:, 0:128], lhsT=kTt.ap[:, cs], rhs=qTt.ap[:, cs], start=True, stop=True), reads=[kTt, qTt], writes=[pS])
            SW = rb128.get()
            k.op("dve", lambda e: e.tensor_tensor(out=SW.ap[:, :], in0=pS.ap[:, 0:128], in1=Wt.ap[:, :], op=ALU.mult), reads=[pS, Wt], writes=[SW])
            Arow = r128.get()
            k.op("act", lambda e: e.activation(out=Arow.ap[0:64, :], in_=M.ap[0:64, :], func=AF.Exp, bias=mprev.ap[0:64, 0:1], scale=-1.0), reads=[M, mprev], writes=[Arow])
            qa = rb128.get()
            k.op("pool", lambda e: e.tensor_tensor(out=qa.ap[0:64, :], in0=qTt.ap[:, cs], in1=Arow.ap[0:64, :], op=ALU.mult), reads=[qTt, Arow], writes=[qa])
            pN = psrot.get(); pD = psrot.get()
            k.op("pe", lambda e: e.matmul(pN.ap[0:64, 0:128], lhsT=STb.ap[:, 0:64], rhs=qa.ap[:, :], start=True, stop=False), reads=[STb, qa], writes=[pN])
            k.op("pe", lambda e: e.matmul(pN.ap[0:64, 0:128], lhsT=vaug2.ap[:, n, 0:64], rhs=SW.ap[:, :], start=False, stop=True), reads=[vaug2, SW], writes=[pN])
            k.op("pe", lambda e: e.matmul(pD.ap[0:64, 0:128], lhsT=STb.ap[:, 64:128], rhs=qa.ap[:, :], start=True, stop=False), reads=[STb, qa], writes=[pD])
            k.op("pe", lambda e: e.matmul(pD.ap[0:64, 0:128], lhsT=vaug2.ap[:, n, 64:128], rhs=SW.ap[:, :], start=False, stop=True), reads=[vaug2, SW], writes=[pD])
            E = r128.get(); aden = r128.get()
            k.op("pool", lambda e: e.tensor_tensor(out=E.ap[0:64, :], in0=Bc.ap[0:64, :], in1=M.ap[0:64, :], op=ALU.add), reads=[Bc, M], writes=[E])
            k.op("act", lambda e: e.activation(out=E.ap[0:64, :], in_=E.ap[0:64, :], func=AF.Exp, scale=-1.0), reads=[E], writes=[E])
            k.op("act", lambda e: e.activation(out=aden.ap[0:64, :], in_=pD.ap[0:64, 0:128], func=AF.Abs), reads=[pD], writes=[aden])
            k.op("dve", lambda e: e.tensor_tensor(out=aden.ap[0:64, :], in0=aden.ap[0:64, :], in1=E.ap[0:64, :], op=ALU.max), reads=[aden, E], writes=[aden])
            k.op("dve", lambda e: e.reciprocal(out=aden.ap[0:64, :], in_=aden.ap[0:64, :]), reads=[aden], writes=[aden])
            k.op("dve", lambda e: e.tensor_tensor(out=hbuf.ap[:, cs], in0=pN.ap[0:64, 0:128], in1=aden.ap[0:64, :], op=ALU.mult), reads=[pN, aden], writes=[hbuf])
            k.op("act", lambda e: e.activation(out=ecol.ap[:, 0:1], in_=M.ap[:, 127:128], func=AF.Exp, bias=ucol.ap[:, 0:1], scale=-1.0), reads=[M, ucol], writes=[ecol])
            wk = rb128.get()
            k.op("dve", lambda e: e.tensor_scalar(out=wk.ap[:, 0:64], in0=big_bf.ap[:, n * 64:(n + 1) * 64], scalar1=ecol.ap[:, 0:1], scalar2=None, op0=ALU.mult), reads=[big_bf, ecol], writes=[wk])
            pU = psrot.get()
            k.op("pe", lambda e: e.matmul(pU.ap[0:64, 0:128], lhsT=wk.ap[:, 0:64], rhs=vaug2.ap[:, n, :], start=True, stop=True), reads=[wk, vaug2], writes=[pU])
            if debug and d == 0 and n == 0:
                pUc = r128.get()
                k.op("act", lambda e: e.activation(out=pUc.ap[0:64, :], in_=pU.ap[0:64, 0:128], func=AF.Copy), reads=[pU], writes=[pUc])
            k.op("dve", lambda e: e.scalar_tensor_tensor(out=ST.ap[:, :], in0=ST.ap[:, :], scalar=Arow.ap[0:64, 127:128], in1=pU.ap[0:64, 0:128], op0=ALU.mult, op1=ALU.add), reads=[ST, Arow, pU], writes=[ST])
            k.op("pool", lambda e: e.tensor_copy(out=STb.ap[0:64, :], in_=ST.ap[:, :]), reads=[ST], writes=[STb])
            if debug and d == 0 and n == 0:
                dbgt = k.sb("dbgt", [128, 1024], F32)
                k.op("pool", lambda e: e.memset(dbgt.ap[:, :], 0.0), writes=[dbgt])
                k.op("dve", lambda e: e.tensor_copy(out=dbgt.ap[0:64, 0:128], in_=ST.ap[:, :]), reads=[ST], writes=[dbgt])
                k.op("dve", lambda e: e.tensor_copy(out=dbgt.ap[:, 128:129], in_=mnew.ap[:, :]), reads=[mnew], writes=[dbgt])
                k.op("dve", lambda e: e.tensor_copy(out=dbgt.ap[:, 129:130], in_=ecol.ap[:, :]), reads=[ecol], writes=[dbgt])
                k.op("dve", lambda e: e.tensor_copy(out=dbgt.ap[:, 130:131], in_=ucol.ap[:, :]), reads=[ucol], writes=[dbgt])
                k.op("dve", lambda e: e.tensor_copy(out=dbgt.ap[:, 256:384], in_=M.ap[:, :]), reads=[M], writes=[dbgt])
                k.op("dve", lambda e: e.tensor_copy(out=dbgt.ap[0:64, 384:512], in_=Arow.ap[0:64, :]), reads=[Arow], writes=[dbgt])
                k.op("dve", lambda e: e.tensor_copy(out=dbgt.ap[0:64, 512:640], in_=pUc.ap[0:64, :]), reads=[pUc], writes=[dbgt])
            if debug and d == 0 and n == 1:
                k.op("dve", lambda e: e.tensor_copy(out=dbgt.ap[:, 640:768], in_=M.ap[:, :]), reads=[M], writes=[dbgt])
                k.op("dve", lambda e: e.tensor_copy(out=dbgt.ap[0:64, 768:896], in_=Arow.ap[0:64, :]), reads=[Arow], writes=[dbgt])
                k.dma("sp", dbg.ap[:, :], dbgt.ap[:, :], reads=[dbgt], writes=[dbg])
        k.dma("sp", o_m.ap[d, :, :], hbuf.ap[:, :], reads=[hbuf], writes=[o_m])


def build_phase_b(debug=False):
    nc = bass.Bass("TRN2", target_bir_lowering=False)
    k = K(nc)
    D = k.dram
    sg_uT = D("sg_uT", [64, TT], BF16, "ExternalInput")
    sg_vtok = D("sg_vtok", [128, NCH * 64], BF16, "ExternalInput")
    sg_wT = D("sg_wT", [128, 128], F32, "ExternalInput")
    sg_bias = D("sg_bias", [1, 128], F32, "ExternalInput")
    da_q = D("da_q", [2, 32, TT], BF16, "ExternalInput")
    da_k = D("da_k", [2, 32, TT], BF16, "ExternalInput")
    da_vtok = D("da_vtok", [128, NCH * 64], BF16, "ExternalInput")
    da_lam = D("da_lam", [1, 128], F32, "ExternalInput")
    da_misc = D("da_misc", [64, 2], F32, "ExternalInput")
    fn_xT = D("fn_xT", [64, TT], BF16, "ExternalInput")
    fn_cs4 = D("fn_cs4", [64, 256], F32, "ExternalInput")
    fn_c1s1 = D("fn_c1s1", [128, 256], F32, "ExternalInput")
    fn_tw = D("fn_tw", [64, 1024], F32, "ExternalInput")
    fn_c2s2 = D("fn_c2s2", [64, 128], F32, "ExternalInput")
    fn_ctx = D("fn_ctx", [128, 2 * 512], F32, "ExternalInput")
    ml_qT = D("ml_qT", [2, 64, TT], BF16, "ExternalInput")
    ml_kT = D("ml_kT", [2, 64, TT], BF16, "ExternalInput")
    ml_ktok = D("ml_ktok", [2, 128, NCH * 64], BF16, "ExternalInput")
    ml_vtok = D("ml_vtok", [2, 128, NCH * 64], BF16, "ExternalInput")
    ml_g = D("ml_g", [2, 2, TT], F32, "ExternalInput")
    ml_const = D("ml_const", [128, 384], F32, "ExternalInput")
    o_s = D("o_s", [64, TT], BF16, "ExternalOutput")
    o_d = D("o_d", [64, TT], BF16, "ExternalOutput")
    o_f = D("o_f", [64, TT], BF16, "ExternalOutput")
    o_m = D("o_m", [2, 64, TT], F32, "ExternalOutput")
    dbg = D("dbg", [128, 1024], F32, "ExternalOutput") if debug else None

    consts(k)
    psrot = PsumRot(k, 8)
    ones_b = k.sb("ones_b", [128, 128], BF16)
    k.op("pool", lambda e: e.memset(ones_b.ap[:, :], 1.0), writes=[ones_b])

    wsf = k.sb("wsf", [128, 128], F32); wsb = k.sb("wsb", [128, 128], BF16)
    k.dma("sp", wsf.ap[:, :], sg_wT.ap[:, :], reads=[sg_wT], writes=[wsf])
    k.op("dve", lambda e: e.tensor_copy(out=wsb.ap[:, :], in_=wsf.ap[:, :]), reads=[wsf], writes=[wsb])
    sb4 = k.sb("sb4", [64, 512], F32)
    for r in range(4):
        k.dma("sp", sb4.ap[:, r * 128:(r + 1) * 128], sg_bias.ap[0:1, :].partition_broadcast(64), reads=[sg_bias], writes=[sb4])
    big_bf = k.sb("big_bf", [128, NCH * 64], BF16)
    rowT = k.sb("rowT", [64, TT], BF16)
    outT = k.sb("outT", [64, TT], BF16)
    k.dma("sp", big_bf.ap[:, :], sg_vtok.ap[:, :], reads=[sg_vtok], writes=[big_bf])
    k.dma("sp", rowT.ap[:, :], sg_uT.ap[:, :], reads=[sg_uT], writes=[rowT])
    t512 = SbRot(k, 3, [128, 512], F32, "t512")
    for g in range(17):
        n0 = g * 4; n1 = min(n0 + 4, NCH); w = (n1 - n0) * 128
        ps = psrot.get()
        for n in range(n0, n1):
            k.op("pe", lambda e, n=n: e.matmul(ps.ap[0:64, (n - n0) * 128:(n - n0 + 1) * 128], lhsT=big_bf.ap[:, n * 64:(n + 1) * 64], rhs=wsb.ap[:, :], start=True, stop=True), reads=[big_bf, wsb], writes=[ps])
        tmp = t512.get()
        k.op("dve", lambda e: e.tensor_tensor(out=tmp.ap[0:64, 0:w], in0=ps.ap[0:64, 0:w], in1=sb4.ap[:, 0:w], op=ALU.add), reads=[ps, sb4], writes=[tmp])
        k.op("pool", lambda e: e.tensor_tensor(out=outT.ap[:, n0 * 128:n0 * 128 + w], in0=tmp.ap[0:64, 0:w], in1=rowT.ap[:, n0 * 128:n0 * 128 + w], op=ALU.mult), reads=[tmp, rowT], writes=[outT])
    k.dma("sp", o_s.ap[:, :], outT.ap[:, :], reads=[outT], writes=[o_s])

    with k.scope():
        def load_cast(name, dr, shape):
            f = k.sb(name + "_f", shape, F32); b_ = k.sb(name + "_b", shape, BF16)
            k.dma("sp", f.ap[:, :], dr.ap[:, :], reads=[dr], writes=[f])
            k.op("pool", lambda e: e.tensor_copy(out=b_.ap[:, :], in_=f.ap[:, :]), reads=[f], writes=[b_])
            return f, b_
        _, cs4 = load_cast("cs4", fn_cs4, [64, 256])
        _, c1s1 = load_cast("c1s1", fn_c1s1, [128, 256])
        tw, _unused = load_cast("tw", fn_tw, [64, 1024])
        _, c2s2 = load_cast("c2s2", fn_c2s2, [64, 128])
        _, fctx = load_cast("fctx", fn_ctx, [128, 1024])
        k.dma("sp", rowT.ap[:, :], fn_xT.ap[:, :], reads=[fn_xT], writes=[rowT])
        Zall = k.sb("Zall", [128, 64, 256], BF16)
        xv = rowT.ap[:, 0:8192].rearrange("c (s1 s2) -> c s2 s1", s2=64)
        for s2 in range(0, 64, 2):
            ps = psrot.get()
            for u in range(2):
                k.op("pe", lambda e, u=u: e.matmul(ps.ap[:, u * 256:(u + 1) * 256], lhsT=xv[:, s2 + u, :], rhs=cs4.ap[:, :], start=True, stop=True), reads=[rowT, cs4], writes=[ps])
            k.op("act", lambda e: e.activation(out=Zall.ap[:, s2:s2 + 2, :], in_=ps.ap[:, 0:512].rearrange("p (u c) -> p u c", u=2), func=AF.Copy), reads=[ps], writes=[Zall])
        Bt = k.sb("Bt", [64, 2, 64, 128], BF16)
        ftmp = SbRot(k, 6, [64, 512], F32, "ftmp")
        for g in range(16):
            pr = psrot.get(); pi = psrot.get()
            for ri, ps in ((0, pr), (1, pi)):
                for u in range(4):
                    m = ri * 64 + g * 4 + u
                    k.op("pe", lambda e, u=u, m=m, ps=ps: e.matmul(ps.ap[0:64, u * 128:(u + 1) * 128], lhsT=Zall.ap[:, :, m], rhs=c1s1.ap[:, 0:128], start=True, stop=False), reads=[Zall, c1s1], writes=[ps])
                    k.op("pe", lambda e, u=u, m=m, ps=ps: e.matmul(ps.ap[0:64, u * 128:(u + 1) * 128], lhsT=Zall.ap[:, :, 128 + m], rhs=c1s1.ap[:, 128:256], start=False, stop=True), reads=[Zall, c1s1], writes=[ps])
            t1 = ftmp.get(); t2 = ftmp.get(); t3 = ftmp.get(); t4 = ftmp.get()
            k.op("dve", lambda e: e.tensor_tensor(out=t1.ap[:, :], in0=pr.ap[0:64, :], in1=tw.ap[:, 0:512], op=ALU.mult), reads=[pr, tw], writes=[t1])
            k.op("dve", lambda e: e.tensor_tensor(out=t2.ap[:, :], in0=pi.ap[0:64, :], in1=tw.ap[:, 512:1024], op=ALU.mult), reads=[pi, tw], writes=[t2])
            k.op("dve", lambda e: e.tensor_tensor(out=t3.ap[:, :], in0=pi.ap[0:64, :], in1=tw.ap[:, 0:512], op=ALU.mult), reads=[pi, tw], writes=[t3])
            k.op("dve", lambda e: e.tensor_tensor(out=t4.ap[:, :], in0=pr.ap[0:64, :], in1=tw.ap[:, 512:1024], op=ALU.mult), reads=[pr, tw], writes=[t4])
            k.op("pool", lambda e: e.tensor_tensor(out=Bt.ap[:, 0, g * 4:(g + 1) * 4, :], in0=t1.ap[:, :].rearrange("p (u k) -> p u k", u=4), in1=t2.ap[:, :].rearrange("p (u k) -> p u k", u=4), op=ALU.add), reads=[t1, t2], writes=[Bt])
            k.op("pool", lambda e: e.tensor_tensor(out=Bt.ap[:, 1, g * 4:(g + 1) * 4, :], in0=t3.ap[:, :].rearrange("p (u k) -> p u k", u=4), in1=t4.ap[:, :].rearrange("p (u k) -> p u k", u=4), op=ALU.subtract), reads=[t3, t4], writes=[Bt])
        for kg in range(16):
            ps = psrot.get()
            for u in range(8):
                k1 = kg * 8 + u
                k.op("pe", lambda e, u=u, k1=k1: e.matmul(ps.ap[0:64, u * 64:(u + 1) * 64], lhsT=Bt.ap[:, 0, :, k1], rhs=c2s2.ap[:, 0:64], start=True, stop=False), reads=[Bt, c2s2], writes=[ps])
                k.op("pe", lambda e, u=u, k1=k1: e.matmul(ps.ap[0:64, u * 64:(u + 1) * 64], lhsT=Bt.ap[:, 1, :, k1], rhs=c2s2.ap[:, 64:128], start=False, stop=True), reads=[Bt, c2s2], writes=[ps])
            dst = outT.ap[:, 0:8192].rearrange("c (k2 k1) -> c k2 k1", k1=128)[:, :, kg * 8:(kg + 1) * 8]
            k.op("act", lambda e, dst=dst: e.activation(out=dst, in_=ps.ap[0:64, 0:512].rearrange("c (u k2) -> c k2 u", u=8), func=AF.Copy), reads=[ps], writes=[outT])
        Uc = k.sb("Uc", [128, 2, 128], BF16)
        for ch in range(2):
            ps = psrot.get()
            k.op("pe", lambda e, ch=ch: e.matmul(ps.ap[:, 0:128], lhsT=rowT.ap[:, 8192 + ch * 128:8192 + (ch + 1) * 128], rhs=cs4.ap[:, 0:128], start=True, stop=True), reads=[rowT, cs4], writes=[ps])
            k.op("act", lambda e, ch=ch: e.activation(out=Uc.ap[:, ch, :], in_=ps.ap[:, 0:128], func=AF.Copy), reads=[ps], writes=[Uc])
        ps = psrot.get()
        for ch in range(2):
            k.op("pe", lambda e, ch=ch: e.matmul(ps.ap[0:64, 0:256], lhsT=Uc.ap[:, ch, 0:64], rhs=fctx.ap[:, ch * 512:ch * 512 + 256], start=(ch == 0), stop=False), reads=[Uc, fctx], writes=[ps])
            k.op("pe", lambda e, ch=ch: e.matmul(ps.ap[0:64, 0:256], lhsT=Uc.ap[:, ch, 64:128], rhs=fctx.ap[:, ch * 512 + 256:ch * 512 + 512], start=False, stop=(ch == 1)), reads=[Uc, fctx], writes=[ps])
        k.op("act", lambda e: e.activation(out=outT.ap[:, 8192:8448], in_=ps.ap[0:64, 0:256], func=AF.Copy), reads=[ps], writes=[outT])
        k.dma("sp", o_f.ap[:, :], outT.ap[:, :], reads=[outT], writes=[o_f])

    with k.scope():
        DSC = 32 ** -0.5
        k.dma("sp", big_bf.ap[:, :], da_vtok.ap[:, :], reads=[da_vtok], writes=[big_bf])
        vaug = k.sb("vaug", [128, NCH, 128], BF16)
        k.op("pool", lambda e: e.memset(vaug.ap[:, :, 64:128], 1.0), writes=[vaug])
        k.op("pool", lambda e: e.tensor_copy(out=vaug.ap[:, :, 0:64], in_=big_bf.ap[:, :].rearrange("p (n d) -> p n d", d=64)), reads=[big_bf], writes=[vaug])
        sel = k.sb("sel", [32, 33], F32)
        k.op("pool", lambda e: e.memset(sel.ap[:, 0:32], 0.0), writes=[sel])
        k.op("pool", lambda e: e.memset(sel.ap[:, 32:33], 1.0), writes=[sel])
        shiftm = k.sb("shiftm", [128, 64], F32)
        k.dma("sp", shiftm.ap[0:64, :], ml_const.ap[0:64, 64:128], reads=[ml_const], writes=[shiftm])
        k.dma("sp", shiftm.ap[64:128, :], ml_const.ap[0:64, 0:64], reads=[ml_const], writes=[shiftm])
        ones64 = k.sb("ones64", [64, 64], F32)
        k.op("pool", lambda e: e.memset(ones64.ap[:, :], 1.0), writes=[ones64])
        lamt = k.sb("lamt", [64, 128], F32); lmisc = k.sb("lmisc", [64, 2], F32)
        k.dma("sp", lamt.ap[:, :], da_lam.ap[0:1, :].partition_broadcast(64), reads=[da_lam], writes=[lamt])
        k.dma("sp", lmisc.ap[:, :], da_misc.ap[:, :], reads=[da_misc], writes=[lmisc])
        lp = k.sb("lp", [64, 64], F32); ls2 = k.sb("ls2", [64, 2], F32); lam_t = k.sb("lam_t", [64, 1], F32)
        k.op("dve", lambda e: e.tensor_tensor(out=lp.ap[:, :].rearrange("p (a d) -> p a d", a=2), in0=lamt.ap[:, :].rearrange("p (a b d) -> p a b d", a=2, b=2)[:, :, 0, :], in1=lamt.ap[:, :].rearrange("p (a b d) -> p a b d", a=2, b=2)[:, :, 1, :], op=ALU.mult), reads=[lamt], writes=[lp])
        k.op("dve", lambda e: e.tensor_reduce(out=ls2.ap[:, :], in_=lp.ap[:, :].rearrange("p (a d) -> p a d", a=2), axis=AX.X, op=ALU.add), reads=[lp], writes=[ls2])
        k.op("act", lambda e: e.activation(out=ls2.ap[:, :], in_=ls2.ap[:, :], func=AF.Exp), reads=[ls2], writes=[ls2])
        k.op("dve", lambda e: e.tensor_tensor(out=lam_t.ap[:, :], in0=ls2.ap[:, 0:1], in1=ls2.ap[:, 1:2], op=ALU.subtract), reads=[ls2], writes=[lam_t])
        k.op("dve", lambda e: e.tensor_scalar(out=lam_t.ap[:, :], in0=lam_t.ap[:, :], scalar1=lmisc.ap[:, 1:2], scalar2=-1.0, op0=ALU.add, op1=ALU.mult), reads=[lam_t, lmisc], writes=[lam_t])
        Qa = [k.sb("Qa%d" % c, [33, TT], BF16) for c in range(2)]
        Ka = [k.sb("Ka%d" % c, [33, TT], BF16) for c in range(2)]
        kmax = k.sb("kmax", [33, 2], F32)
        nrm = k.sb("nrm", [33, TT], F32)
        for c in range(2):
            k.dma("sp", Qa[c].ap[0:32, :], da_q.ap[c, :, :], reads=[da_q], writes=[Qa[c]])
            k.dma("sp", Ka[c].ap[0:32, :], da_k.ap[c, :, :], reads=[da_k], writes=[Ka[c]])
            k.op("pool", lambda e, c=c: e.memset(Ka[c].ap[32:33, :], 1.0), writes=[Ka[c]])
        for c in range(2):
            for which, src in ((0, Ka[c]), (1, Qa[c])):
                for g in range(17):
                    c0 = g * 512; n = min(512, TT - c0)
                    sq = t512.get(); ps = psrot.get()
                    k.op("act", lambda e: e.activation(out=sq.ap[0:32, 0:n], in_=src.ap[0:32, c0:c0 + n], func=AF.Square), reads=[src], writes=[sq])
                    k.op("pe", lambda e: e.matmul(ps.ap[0:33, 0:n], lhsT=sel.ap[:, :], rhs=sq.ap[0:32, 0:n], start=True, stop=True), reads=[sel, sq], writes=[ps])
                    if which == 0:
                        k.op("act", lambda e: e.activation(out=nrm.ap[32:33, c0:c0 + n], in_=ps.ap[32:33, 0:n], func=AF.Copy), reads=[ps], writes=[nrm])
                    else:
                        k.op("act", lambda e: e.activation(out=nrm.ap[32:33, c0:c0 + n], in_=ps.ap[32:33, 0:n], func=AF.Sqrt, scale=kmax.ap[32:33, c:c + 1]), reads=[ps, kmax], writes=[nrm])
                        k.op("dve", lambda e: e.tensor_scalar(out=Qa[c].ap[32:33, c0:c0 + n], in0=nrm.ap[32:33, c0:c0 + n], scalar1=-1.0, scalar2=None, op0=ALU.mult), reads=[nrm], writes=[Qa[c]])
                if which == 0:
                    k.op("dve", lambda e: e.reduce_max(out=kmax.ap[32:33, c:c + 1], in_=nrm.ap[32:33, :], axis=AX.X), reads=[nrm], writes=[kmax])
        Prot = SbRot(k, 3, [128, 512], BF16, "Pexp")
        osb = SbRot(k, 2, [128, 512], F32, "osb")
        onrm = SbRot(k, 2, [64, 512], F32, "onrm")
        psS = [k_ for k_ in psrot.tiles[0:3]]
        psO = psrot.tiles[3:5]
        psX = psrot.tiles[5:8]
        si = 0; xi = 0
        for qt in range(17):
            q0 = qt * 512; nq = min(512, TT - q0)
            kts = list(range(NCH)) if qt < 16 else [64, 65]
            on = []
            for c in range(2):
                po = psO[c]
                for ii, kt in enumerate(kts):
                    pS = psS[si % 3]; si += 1
                    k.op("pe", lambda e: e.matmul(pS.ap[:, 0:nq], lhsT=Ka[c].ap[:, kt * 128:(kt + 1) * 128], rhs=Qa[c].ap[:, q0:q0 + nq], start=True, stop=True), reads=[Ka[c], Qa[c]], writes=[pS])
                    P = Prot.get()
                    k.op("act", lambda e: e.activation(out=P.ap[:, 0:nq], in_=pS.ap[:, 0:nq], func=AF.Exp, scale=DSC), reads=[pS], writes=[P])
                    k.op("pe", lambda e: e.matmul(po.ap[:, 0:nq], lhsT=vaug.ap[:, kt, :], rhs=P.ap[:, 0:nq], start=(ii == 0), stop=(ii == len(kts) - 1)), reads=[vaug, P], writes=[po])
                ob = osb.get()
                k.op("dve", lambda e: e.tensor_copy(out=ob.ap[:, 0:nq], in_=po.ap[:, 0:nq]), reads=[po], writes=[ob])
                pz = psX[xi % 3]; xi += 1
                k.op("pe", lambda e: e.matmul(pz.ap[0:64, 0:nq], lhsT=shiftm.ap[:, :], rhs=ob.ap[:, 0:nq], start=True, stop=True), reads=[shiftm, ob], writes=[pz])
                rz = t512.get()
                k.op("dve", lambda e: e.reciprocal(out=rz.ap[0:64, 0:nq], in_=pz.ap[0:64, 0:nq]), reads=[pz], writes=[rz])
                o_n = onrm.get()
                k.op("pool", lambda e: e.tensor_tensor(out=o_n.ap[:, 0:nq], in0=ob.ap[0:64, 0:nq], in1=rz.ap[0:64, 0:nq], op=ALU.mult), reads=[ob, rz], writes=[o_n])
                on.append(o_n)
            od = t512.get()
            k.op("dve", lambda e: e.scalar_tensor_tensor(out=od.ap[0:64, 0:nq], in0=on[1].ap[:, 0:nq], scalar=lam_t.ap[:, 0:1], in1=on[0].ap[:, 0:nq], op0=ALU.mult, op1=ALU.add), reads=[on[0], on[1], lam_t], writes=[od])
            sq = t512.get()
            k.op("pool", lambda e: e.tensor_tensor(out=sq.ap[0:64, 0:nq], in0=od.ap[0:64, 0:nq], in1=od.ap[0:64, 0:nq], op=ALU.mult), reads=[od], writes=[sq])
            pss = psX[xi % 3]; xi += 1
            k.op("pe", lambda e: e.matmul(pss.ap[0:64, 0:nq], lhsT=ones64.ap[:, :], rhs=sq.ap[0:64, 0:nq], start=True, stop=True), reads=[ones64, sq], writes=[pss])
            rs_ = t512.get()
            k.op("act", lambda e: e.activation(out=rs_.ap[0:64, 0:nq], in_=pss.ap[0:64, 0:nq], func=AF.Sqrt, bias=k.eps_ap[0:64, :], scale=1.0 / 64), reads=[pss], writes=[rs_])
            k.op("dve", lambda e: e.reciprocal(out=rs_.ap[0:64, 0:nq], in_=rs_.ap[0:64, 0:nq]), reads=[rs_], writes=[rs_])
            k.op("dve", lambda e: e.scalar_tensor_tensor(out=outT.ap[:, q0:q0 + nq], in0=od.ap[0:64, 0:nq], scalar=lmisc.ap[:, 0:1], in1=rs_.ap[0:64, 0:nq], op0=ALU.mult, op1=ALU.mult), reads=[od, lmisc, rs_], writes=[outT])
        k.dma("sp", o_d.ap[:, :], outT.ap[:, :], reads=[outT], writes=[o_d])

    with k.scope():
        emit_mlstm(k, psrot, big_bf, ml_qT, ml_kT, ml_ktok, ml_vtok, ml_g, ml_const, o_m, debug=debug, dbg=dbg)
    k.finish([o_s, o_f, o_d, o_m])
    return nc


def build_phase_c():
    nc = bass.Bass("TRN2", target_bir_lowering=False)
    k = K(nc)
    D = k.dram
    xT = D("xT", [1024, NT], F32, "ExternalInput")
    hT = D("hT", [1024, NT], BF16, "ExternalInput")
    modI = D("modI", [128, 96], F32, "ExternalInput")
    brT = D("brT", [768, NT], BF16, "ExternalInput")
    mfT = D("mfT", [256, NT], F32, "ExternalInput")
    mbT = D("mbT", [256, NT], F32, "ExternalInput")
    oT = D("oT", [256, NT], BF16, "ExternalInput")
    wG = D("wG", [1024, 4096], F32, "ExternalInput")
    wBr = D("wBr", [1024, 1024], F32, "ExternalInput")
    wOut = D("wOut", [1024, 1024], F32, "ExternalInput")
    wUp = D("wUp", [1024, 5632], F32, "ExternalInput")
    wDn = D("wDn", [2816, 1024], F32, "ExternalInput")
    fcw = D("fcw", [128, 66], F32, "ExternalInput")
    fcb = D("fcb", [128, 22], F32, "ExternalInput")
    ngI = D("ngI", [128, 24], F32, "ExternalInput")
    mlnI = D("mlnI", [128, 2], F32, "ExternalInput")
    cmask = D("cmask", [1, NT], F32, "ExternalInput")
    bdI = D("bdI", [128, 128], F32, "ExternalInput")
    xoT = D("xoT", [1024, NT], F32, "ExternalOutput")
    mgD = D("mgD", [1024, NT], BF16, "Internal")
    yD = D("yD", [1024, NT], F32, "Internal")
    xmD = D("xmD", [1024, NT], F32, "Internal")
    uD = D("uD", [2816, NT], BF16, "Internal")
    yfD = D("yfD", [1024, NT], F32, "Internal")

    consts(k)
    psrot = PsumRot(k, 8)
    sqrot = SbRot(k, 3, [128, 512], F32, "sq")
    xrot = SbRot(k, 4, [128, 512], F32, "xr")
    t512 = SbRot(k, 4, [128, 512], F32, "t512")
    wfrot = SbRot(k, 4, [128, 8, 128], F32, "wf")
    wbrot = SbRot(k, 4, [128, 8, 128], BF16, "wb")
    modT = k.sb("modT", [128, 48, 2], F32)
    k.dma("sp", modT.ap[:, :, :].rearrange("p a b -> p (a b)"), modI.ap[:, :], reads=[modI], writes=[modT])
    ng = k.sb("ng", [128, 24], F32)
    k.dma("sp", ng.ap[:, :], ngI.ap[:, :], reads=[ngI], writes=[ng])
    mln = k.sb("mln", [128, 2], F32)
    k.dma("sp", mln.ap[:, :], mlnI.ap[:, :], reads=[mlnI], writes=[mln])
    bd = k.sb("bd", [128, 128], F32)
    k.dma("sp", bd.ap[:, :], bdI.ap[:, :], reads=[bdI], writes=[bd])
    mask = k.sb("mask", [128, NT], F32)
    k.dma("sp", mask.ap[:, :], cmask.ap[0:1, :].partition_broadcast(128), reads=[cmask], writes=[mask])
    cw = k.sb("cw", [128, 66], F32); cb = k.sb("cb", [128, 22], F32)
    k.dma("sp", cw.ap[:, :], fcw.ap[:, :], reads=[fcw], writes=[cw])
    k.dma("sp", cb.ap[:, :], fcb.ap[:, :], reads=[fcb], writes=[cb])
    G1 = k.sb("G1", [128, 8, 2], F32); A2 = k.sb("A2", [128, 8, 2], F32); G2 = k.sb("G2", [128, 8, 2], F32)
    for kc in range(8):
        k.op("dve", lambda e, kc=kc: e.tensor_scalar(out=G1.ap[:, kc, :], in0=modT.ap[:, 16 + kc, :], scalar1=ng.ap[:, kc:kc + 1], scalar2=None, op0=ALU.mult), reads=[modT, ng], writes=[G1])
        k.op("dve", lambda e, kc=kc: e.tensor_scalar(out=A2.ap[:, kc, :], in0=modT.ap[:, 32 + kc, :], scalar1=1.0, scalar2=ng.ap[:, 8 + kc:9 + kc], op0=ALU.add, op1=ALU.mult), reads=[modT, ng], writes=[A2])
        k.op("dve", lambda e, kc=kc: e.tensor_scalar(out=G2.ap[:, kc, :], in0=modT.ap[:, 40 + kc, :], scalar1=ng.ap[:, 16 + kc:17 + kc], scalar2=None, op0=ALU.mult), reads=[modT, ng], writes=[G2])
    rstd = k.sb("rstd", [128, NT], F32)
    h = k.sb("h", [128, 8, NT], BF16)
    for kc in range(8):
        k.dma("sp", h.ap[:, kc, :], hT.ap[kc * 128:(kc + 1) * 128, :], reads=[hT], writes=[h])

    def wload(wd, col0, nk=8, row0=0):
        wf = wfrot.get(); wb = wbrot.get()
        src = wd.ap[row0:row0 + nk * 128, col0:col0 + 128].rearrange("(kc p) c -> p kc c", p=128)
        k.dma("sp", wf.ap[:, 0:nk, :], src, reads=[wd], writes=[wf])
        k.op("pool", lambda e: e.tensor_copy(out=wb.ap[:, 0:nk, :], in_=wf.ap[:, 0:nk, :]), reads=[wf], writes=[wb])
        return wb

    with k.scope():
        br = k.sb("br", [128, 8, NT], BF16)
        orot_c = SbRot(k, 2, [128, 512], BF16, "orotc")
        for kc in range(6):
            k.dma("sp", br.ap[:, 2 + kc, :], brT.ap[kc * 128:(kc + 1) * 128, :], reads=[brT], writes=[br])
        for mc_ in range(2):
            for (c0, c1) in CTS:
                n = c1 - c0
                a_ = xrot.get(); b_ = xrot.get(); ot = sqrot.get()
                k.dma("sp", a_.ap[:, 0:n], mfT.ap[mc_ * 128:(mc_ + 1) * 128, c0:c1], reads=[mfT], writes=[a_])
                k.dma("sp", b_.ap[:, 0:n], mbT.ap[mc_ * 128:(mc_ + 1) * 128, c0:c1], reads=[mbT], writes=[b_])
                k.op("dve", lambda e: e.tensor_tensor(out=a_.ap[:, 0:n], in0=a_.ap[:, 0:n], in1=b_.ap[:, 0:n], op=ALU.add), reads=[a_, b_], writes=[a_])
                k.op("pool", lambda e: e.tensor_tensor(out=b_.ap[:, 0:n], in0=a_.ap[:, 0:n], in1=a_.ap[:, 0:n], op=ALU.mult), reads=[a_], writes=[b_])
                ps = psrot.get()
                k.op("pe", lambda e: e.matmul(ps.ap[:, 0:n], lhsT=bd.ap[:, :], rhs=b_.ap[:, 0:n], start=True, stop=True), reads=[bd, b_], writes=[ps])
                r_ = t512.get()
                k.op("act", lambda e: e.activation(out=r_.ap[:, 0:n], in_=ps.ap[:, 0:n], func=AF.Sqrt, bias=k.eps_ap, scale=1.0 / 64), reads=[ps], writes=[r_])
                k.op("dve", lambda e: e.reciprocal(out=r_.ap[:, 0:n], in_=r_.ap[:, 0:n]), reads=[r_], writes=[r_])
                k.op("dve", lambda e: e.scalar_tensor_tensor(out=a_.ap[:, 0:n], in0=a_.ap[:, 0:n], scalar=mln.ap[:, mc_:mc_ + 1], in1=r_.ap[:, 0:n], op0=ALU.mult, op1=ALU.mult), reads=[a_, mln, r_], writes=[a_])
                og = t512.get()
                osb_ = orot_c.get()
                k.dma("sp", osb_.ap[:, 0:n], oT.ap[mc_ * 128:(mc_ + 1) * 128, c0:c1], reads=[oT], writes=[osb_])
                k.op("act", lambda e: e.activation(out=og.ap[:, 0:n], in_=osb_.ap[:, 0:n], func=AF.Sigmoid), reads=[osb_], writes=[og])
                k.op("dve", lambda e: e.tensor_tensor(out=br.ap[:, mc_, c0:c1], in0=a_.ap[:, 0:n], in1=og.ap[:, 0:n], op=ALU.mult), reads=[a_, og], writes=[br])
        accrot = SbRot(k, 2, [128, NT], F32, "acc")
        mgst = SbRot(k, 2, [128, NT], BF16, "mgst")
        for ob in range(8):
            acc = accrot.get()
            for g in range(4):
                wg_ = wload(wG, g * 1024 + ob * 128)
                wb_ = wload(wBr, ob * 128, nk=2, row0=g * 256)
                for (c0, c1) in CTS:
                    n = c1 - c0
                    psG = psrot.get(); psY = psrot.get()
                    for kc in range(8):
                        k.op("pe", lambda e, kc=kc: e.matmul(psG.ap[:, 0:n], lhsT=wg_.ap[:, kc, :], rhs=h.ap[:, kc, c0:c1], start=(kc == 0), stop=(kc == 7)), reads=[wg_, h], writes=[psG])
                    for kc in range(2):
                        k.op("pe", lambda e, kc=kc: e.matmul(psY.ap[:, 0:n], lhsT=wb_.ap[:, kc, :], rhs=br.ap[:, g * 2 + kc, c0:c1], start=(kc == 0), stop=(kc == 1)), reads=[wb_, br], writes=[psY])
                    sg_ = t512.get()
                    k.op("act", lambda e: e.activation(out=sg_.ap[:, 0:n], in_=psG.ap[:, 0:n], func=AF.Sigmoid), reads=[psG], writes=[sg_])
                    if g == 0:
                        k.op("dve", lambda e: e.tensor_tensor(out=acc.ap[:, c0:c1], in0=psY.ap[:, 0:n], in1=sg_.ap[:, 0:n], op=ALU.mult), reads=[psY, sg_], writes=[acc])
                    else:
                        k.op("dve", lambda e: e.tensor_tensor(out=sg_.ap[:, 0:n], in0=psY.ap[:, 0:n], in1=sg_.ap[:, 0:n], op=ALU.mult), reads=[psY, sg_], writes=[sg_])
                        k.op("pool", lambda e: e.tensor_tensor(out=acc.ap[:, c0:c1], in0=acc.ap[:, c0:c1], in1=sg_.ap[:, 0:n], op=ALU.add), reads=[acc, sg_], writes=[acc])
            mg_ = mgst.get()
            k.op("act", lambda e: e.activation(out=mg_.ap[:, :], in_=acc.ap[:, :], func=AF.Copy), reads=[acc], writes=[mg_])
            k.dma("sp", mgD.ap[ob * 128:(ob + 1) * 128, :], mg_.ap[:, :], reads=[mg_], writes=[mgD])
        for kc in range(8):
            k.dma("sp", br.ap[:, kc, :], mgD.ap[kc * 128:(kc + 1) * 128, :], reads=[mgD], writes=[br])
        for ob in range(8):
            wo_ = wload(wOut, ob * 128)
            for (c0, c1) in CTS:
                n = c1 - c0
                ps = psrot.get()
                for kc in range(8):
                    k.op("pe", lambda e, kc=kc: e.matmul(ps.ap[:, 0:n], lhsT=wo_.ap[:, kc, :], rhs=br.ap[:, kc, c0:c1], start=(kc == 0), stop=(kc == 7)), reads=[wo_, br], writes=[ps])
                st = t512.get()
                k.op("act", lambda e: e.activation(out=st.ap[:, 0:n], in_=ps.ap[:, 0:n], func=AF.Copy), reads=[ps], writes=[st])
                k.dma("sp", yD.ap[ob * 128:(ob + 1) * 128, c0:c1], st.ap[:, 0:n], reads=[st], writes=[yD])

    def resid(srcY, srcX, G, dst):
        stream_rstd(k, srcY, 8, rstd, psrot, xrot, sqrot, 1.0 / 1024)
        for (c0, c1) in CTS:
            n = c1 - c0
            for kc in range(8):
                yt = xrot.get(); xt = xrot.get()
                k.dma("sp", yt.ap[:, 0:n], srcY.ap[kc * 128:(kc + 1) * 128, c0:c1], reads=[srcY], writes=[yt])
                k.dma("sp", xt.ap[:, 0:n], srcX.ap[kc * 128:(kc + 1) * 128, c0:c1], reads=[srcX], writes=[xt])
                for (si, a, b) in seg_split(c0, c1):
                    k.op("dve", lambda e, si=si, a=a, b=b: e.scalar_tensor_tensor(out=yt.ap[:, a - c0:b - c0], in0=yt.ap[:, a - c0:b - c0], scalar=G.ap[:, kc, si:si + 1], in1=rstd.ap[:, a:b], op0=ALU.mult, op1=ALU.mult), reads=[yt, G, rstd], writes=[yt])
                k.op("pool", lambda e: e.tensor_tensor(out=xt.ap[:, 0:n], in0=xt.ap[:, 0:n], in1=yt.ap[:, 0:n], op=ALU.add), reads=[xt, yt], writes=[xt])
                k.dma("sp", dst.ap[kc * 128:(kc + 1) * 128, c0:c1], xt.ap[:, 0:n], reads=[xt], writes=[dst])

    resid(yD, xT, G1, xmD)
    stream_rstd(k, xmD, 8, rstd, psrot, xrot, sqrot, 1.0 / 1024)
    tmprot = SbRot(k, 2, [128, 512], F32, "tmpn")
    stream_modnorm(k, xmD, rstd, A2, modT, 24, h, xrot, tmprot)

    with k.scope():
        a_sb = k.sb("a_sb", [128, NT], F32); cv = k.sb("cv", [128, NT], F32)
        g_sb = k.sb("g_sb", [128, NT], F32)
        urot = SbRot(k, 2, [128, NT], BF16, "urot")
        for fb in range(22):
            wa = wload(wUp, fb * 128); wg2 = wload(wUp, 2816 + fb * 128)
            for (c0, c1) in CTS:
                n = c1 - c0
                pa = psrot.get(); pg = psrot.get()
                for kc in range(8):
                    k.op("pe", lambda e, kc=kc: e.matmul(pa.ap[:, 0:n], lhsT=wa.ap[:, kc, :], rhs=h.ap[:, kc, c0:c1], start=(kc == 0), stop=(kc == 7)), reads=[wa, h], writes=[pa])
                for kc in range(8):
                    k.op("pe", lambda e, kc=kc: e.matmul(pg.ap[:, 0:n], lhsT=wg2.ap[:, kc, :], rhs=h.ap[:, kc, c0:c1], start=(kc == 0), stop=(kc == 7)), reads=[wg2, h], writes=[pg])
                k.op("dve", lambda e: e.tensor_tensor(out=a_sb.ap[:, c0:c1], in0=pa.ap[:, 0:n], in1=mask.ap[:, c0:c1], op=ALU.mult), reads=[pa, mask], writes=[a_sb])
                k.op("act", lambda e: e.activation(out=g_sb.ap[:, c0:c1], in_=pg.ap[:, 0:n], func=AF.Copy), reads=[pg], writes=[g_sb])
            n = NT - 2
            k.op("dve", lambda e: e.tensor_scalar(out=cv.ap[:, 1:1 + n], in0=a_sb.ap[:, 1:1 + n], scalar1=cw.ap[:, fb * 3 + 1:fb * 3 + 2], scalar2=cb.ap[:, fb:fb + 1], op0=ALU.mult, op1=ALU.add), reads=[a_sb, cw, cb], writes=[cv])
            k.op("dve", lambda e: e.scalar_tensor_tensor(out=cv.ap[:, 1:1 + n], in0=a_sb.ap[:, 0:n], scalar=cw.ap[:, fb * 3:fb * 3 + 1], in1=cv.ap[:, 1:1 + n], op0=ALU.mult, op1=ALU.add), reads=[a_sb, cw, cv], writes=[cv])
            k.op("dve", lambda e: e.scalar_tensor_tensor(out=cv.ap[:, 1:1 + n], in0=a_sb.ap[:, 2:2 + n], scalar=cw.ap[:, fb * 3 + 2:fb * 3 + 3], in1=cv.ap[:, 1:1 + n], op0=ALU.mult, op1=ALU.add), reads=[a_sb, cw, cv], writes=[cv])
            k.op("act", lambda e: e.activation(out=cv.ap[:, 1:1 + n], in_=cv.ap[:, 1:1 + n], func=AF.Silu), reads=[cv], writes=[cv])
            u_ = urot.get()
            k.op("pool", lambda e: e.memset(u_.ap[:, :], 0.0), writes=[u_])
            k.op("pool", lambda e: e.tensor_tensor(out=u_.ap[:, 1:1 + n], in0=cv.ap[:, 1:1 + n], in1=g_sb.ap[:, 1:1 + n], op=ALU.mult), reads=[cv, g_sb], writes=[u_])
            k.dma("sp", uD.ap[fb * 128:(fb + 1) * 128, :], u_.ap[:, :], reads=[u_], writes=[uD])
    with k.scope():
        wd = k.sb("wd", [128, 22, 1024], BF16)
        wdf = SbRot(k, 2, [128, 1024], F32, "wdf")
        for fb in range(22):
            f_ = wdf.get()
            k.dma("sp", f_.ap[:, :], wDn.ap[fb * 128:(fb + 1) * 128, :], reads=[wDn], writes=[f_])
            k.op("pool", lambda e: e.tensor_copy(out=wd.ap[:, fb, :], in_=f_.ap[:, :]), reads=[f_], writes=[wd])
        ut = SbRot(k, 2, [128, 22, 512], BF16, "ut")
        for (c0, c1) in CTS:
            n = c1 - c0
            u_ = ut.get()
            k.dma("sp", u_.ap[:, :, 0:n], uD.ap[:, c0:c1].rearrange("(fb p) c -> p fb c", p=128), reads=[uD], writes=[u_])
            for ob in range(8):
                ps = psrot.get()
                for fb in range(22):
                    k.op("pe", lambda e, fb=fb: e.matmul(ps.ap[:, 0:n], lhsT=wd.ap[:, fb, ob * 128:(ob + 1) * 128], rhs=u_.ap[:, fb, 0:n], start=(fb == 0), stop=(fb == 21)), reads=[wd, u_], writes=[ps])
                st = t512.get()
                k.op("act", lambda e: e.activation(out=st.ap[:, 0:n], in_=ps.ap[:, 0:n], func=AF.Copy), reads=[ps], writes=[st])
                k.dma("sp", yfD.ap[ob * 128:(ob + 1) * 128, c0:c1], st.ap[:, 0:n], reads=[st], writes=[yfD])
    resid(yfD, xmD, G2, xoT)
    k.finish([xoT])
    return nc

import numpy as np, math
import ml_dtypes
BF = ml_dtypes.bfloat16
NT = 2315
OFF = dict(P0=0, P1=512, P2=768, P3=1024, P4=1040, P5=1296, P6=1552, P7=1808, P8=2064, P9=2320, P10=2576)


def col_tokens(q):
    t = q * 2048 - 4 + np.arange(2056)
    valid = (t >= 0) & (t < 8192)
    return t, valid


def to_cols(xb, xcb, q):
    F = xb.shape[1]
    out = np.zeros((NT, F), xb.dtype)
    t, valid = col_tokens(q)
    out[1:2057][valid] = xb[t[valid]]
    out[2058:2314] = xcb
    return out


def cmask_for(q):
    m = np.zeros((1, NT), np.float32)
    t, valid = col_tokens(q)
    m[0, 1:2057] = valid
    m[0, 2058:2314] = 1
    return m


def rope_tables(q):
    C = np.ones((128, NT), np.float32); S = np.zeros((128, NT), np.float32)
    t, valid = col_tokens(q)
    tt = np.where(valid, t, 0).astype(np.float32)
    rows = np.floor(tt / 64); cols = tt - rows * 64
    p = np.arange(128); d = p % 32
    axis = d // 16; half = (d % 16) // 8; i = d % 8
    freq = (10000.0 ** (-i.astype(np.float32) / 8)).astype(np.float32)
    pos = np.where(axis[:, None] == 0, rows[None, :], cols[None, :]).astype(np.float32)
    ang = (pos * freq[:, None]).astype(np.float32)
    C[:, 1:2057] = np.cos(ang); S[:, 1:2057] = np.sin(ang) * np.where(half[:, None] == 0, -1.0, 1.0)
    return C, S


def partner64():
    dd = np.arange(64)
    c = dd // 32; d = dd % 32; axis = d // 16; half = (d % 16) // 8; i = d % 8
    return c * 32 + axis * 16 + (1 - half) * 8 + i


def build_wA(w_in_l):
    blocks = []
    O = OFF
    for j in range(4):
        blocks.append(np.concatenate([w_in_l[:, O['P0'] + 64 * j:O['P0'] + 64 * j + 64], w_in_l[:, O['P0'] + 256 + 64 * j:O['P0'] + 256 + 64 * j + 64]], 1))
    for j in range(4):
        blocks.append(np.concatenate([w_in_l[:, O['P1'] + 64 * j:O['P1'] + 64 * j + 64], w_in_l[:, O['P2'] + 64 * j:O['P2'] + 64 * j + 64]], 1))
    for j in range(4):
        blocks.append(np.concatenate([w_in_l[:, O['P4'] + 64 * j:O['P4'] + 64 * j + 64], w_in_l[:, O['P5'] + 64 * j:O['P5'] + 64 * j + 64]], 1))
    pr = partner64()
    for j in range(4):
        blocks.append(np.concatenate([w_in_l[:, O['P4'] + 64 * j + pr], w_in_l[:, O['P5'] + 64 * j + pr]], 1))
    for j in range(4):
        blocks.append(np.concatenate([w_in_l[:, O['P6'] + 64 * j:O['P6'] + 64 * j + 64], w_in_l[:, O['P7'] + 64 * j:O['P7'] + 64 * j + 64]], 1))
    blocks.append(w_in_l[:, O['P8']:O['P8'] + 128]); blocks.append(w_in_l[:, O['P8'] + 128:O['P8'] + 256])
    blocks.append(w_in_l[:, O['P9']:O['P9'] + 128]); blocks.append(w_in_l[:, O['P9'] + 128:O['P9'] + 256])
    g = np.zeros((1024, 128), np.float32); g[:, :16] = w_in_l[:, O['P3']:O['P3'] + 16]
    blocks.append(g)
    return np.ascontiguousarray(np.concatenate(blocks, 1))


def phase_a_inputs(inp, l, x_full, xc_full):
    wA = build_wA(inp['w_in'][l])
    w_ada = np.ascontiguousarray(inp['w_ada'][l])
    b_adaT = np.ascontiguousarray(inp['b_ada'][l].reshape(48, 128).T)
    ng0 = np.ascontiguousarray(inp['norm_g'][l, 0].reshape(8, 128).T)
    mlcw = np.zeros((128, 12), np.float32); mlcb = np.zeros((128, 4), np.float32)
    p = np.arange(128)
    for j in range(4):
        col = np.where(p < 64, 64 * j + p, 256 + 64 * j + (p - 64))
        for tap in range(3):
            mlcw[:, j * 3 + tap] = inp['ml_conv_w'][l, tap, col]
        mlcb[:, j] = inp['ml_conv_b'][l, col]
    qksc = np.where(p < 64, 0.125, 1.0).astype(np.float32).reshape(128, 1)
    gateb = np.zeros((16, 2), np.float32)
    gateb[:, 0] = inp['ml_gate_b'][l].reshape(16)
    gateb[:, 1] = ((np.arange(16) // 4) % 2 == 1)
    sgn = np.ascontiguousarray(inp['sg_norm'][l].reshape(2, 128).T)
    maps = []
    for i in range(8):
        b, q = i // 4, i % 4
        xT = np.ascontiguousarray(to_cols(x_full[b], xc_full[b], q).T)
        cT = np.ascontiguousarray(np.stack([inp['c'][b], inp['c_ctx']], 1))
        C, S = rope_tables(q)
        maps.append(dict(xT=xT, cT=cT, w_ada=w_ada, b_adaT=b_adaT, ng0=ng0, wA=wA, cmask=cmask_for(q), ropeC=C, ropeS=S,
                         mlcw=mlcw, mlcb=mlcb, qksc=qksc, gateb=gateb, sgn=sgn))
    return maps


TT = 8448
NCH = 66


def gather_PT(resA, b):
    PTf = np.zeros((20 * 128, TT), BF); GTf = np.zeros((16, TT), np.float32)
    for q in range(4):
        r = resA[b * 4 + q]
        PTf[:, q * 2048:(q + 1) * 2048] = np.asarray(r["PT"])[:, 5:2053]
        GTf[:, q * 2048:(q + 1) * 2048] = np.asarray(r["GT"])[:, 5:2053]
    r = resA[b * 4]
    PTf[:, 8192:] = np.asarray(r["PT"])[:, 2058:2314]
    GTf[:, 8192:] = np.asarray(r["GT"])[:, 2058:2314]
    return PTf, GTf


def tokmajor(xT):
    F = xT.shape[0]
    return np.ascontiguousarray(xT.T.reshape(NCH, 128, F).transpose(1, 0, 2).reshape(128, NCH * F))


def fourier_tables():
    c = np.arange(64, dtype=np.float64)
    a = 2 * np.pi * np.outer(c, c) / 64
    Cc, Sc = np.cos(a), np.sin(a)
    cs4 = np.concatenate([Cc, -Sc, -Sc, -Cc], 1).astype(np.float32)
    s1 = np.arange(128, dtype=np.float64)
    a1 = 2 * np.pi * np.outer(s1, s1) / 128
    c1s1 = np.concatenate([np.cos(a1), np.sin(a1)], 1).astype(np.float32)
    th = 2 * np.pi * np.outer(c, s1) / 8192
    tw = np.concatenate([np.tile(np.cos(th), (1, 4)), np.tile(np.sin(th), (1, 4))], 1).astype(np.float32)
    sc = 1.0 / math.sqrt(8192 * 64)
    a2 = 2 * np.pi * np.outer(c, c) / 64
    c2s2 = (np.concatenate([np.cos(a2), np.sin(a2)], 1) * sc).astype(np.float32)
    s = np.arange(256, dtype=np.float64)
    a3 = 2 * np.pi * np.outer(s, s) / 256
    sc2 = 1.0 / math.sqrt(256 * 64)
    fctx = np.zeros((128, 1024), np.float32)
    for ch in range(2):
        fctx[:, ch * 512:ch * 512 + 256] = np.cos(a3[ch * 128:(ch + 1) * 128]) * sc2
        fctx[:, ch * 512 + 256:ch * 512 + 512] = np.sin(a3[ch * 128:(ch + 1) * 128]) * sc2
    return dict(fn_cs4=cs4, fn_c1s1=c1s1, fn_tw=tw, fn_c2s2=c2s2, fn_ctx=fctx)


def ml_consts():
    ident = np.eye(128, dtype=np.float32)
    s = np.arange(128)
    big = (s[:, None] > s[None, :]).astype(np.float32) * 1e4
    return np.ascontiguousarray(np.concatenate([ident, big, np.ones((128, 128), np.float32)], 1))


def phase_b_inputs(inp, l, resA):
    lam_init = 0.8 - 0.6 * math.exp(-0.3 * l)
    ft = fourier_tables()
    mlc = ml_consts()
    maps = []
    for b in range(2):
        PTf, GTf = gather_PT(resA, b)
        blk = lambda i, half: PTf[i * 128 + half * 64:i * 128 + half * 64 + 64]
        for j in range(4):
            m = {}
            m["sg_uT"] = np.ascontiguousarray(blk(16 + j // 2, j % 2))
            m["sg_vtok"] = tokmajor(blk(18 + j // 2, j % 2))
            m["sg_wT"] = np.ascontiguousarray(inp["sg_w"][l, j].T)
            m["sg_bias"] = np.ascontiguousarray(inp["sg_b"][l, j][None, :])
            daq = blk(8 + j, 0); dak = blk(8 + j, 1)
            m["da_q"] = np.ascontiguousarray(daq.reshape(2, 32, TT)); m["da_k"] = np.ascontiguousarray(dak.reshape(2, 32, TT))
            m["da_vtok"] = tokmajor(blk(12 + j, 0))
            m["da_lam"] = np.ascontiguousarray(inp["da_lam"][l].reshape(1, 128))
            misc = np.zeros((64, 2), np.float32); misc[:, 0] = inp["da_subln"][l] * (1.0 - lam_init); misc[:, 1] = lam_init
            m["da_misc"] = misc
            m["fn_xT"] = np.ascontiguousarray(blk(12 + j, 1))
            m.update(ft)
            def seq(xT, d):
                lat, cx = xT[:, :8192], xT[:, 8192:]
                if d == 1:
                    lat, cx = lat[:, ::-1], cx[:, ::-1]
                return np.concatenate([cx, lat], 1)
            q_, k_, v_ = blk(j, 0), blk(j, 1), blk(4 + j, 0)
            m["ml_qT"] = np.ascontiguousarray(np.stack([seq(q_, 0), seq(q_, 1)]))
            m["ml_kT"] = np.ascontiguousarray(np.stack([seq(k_, 0), seq(k_, 1)]))
            m["ml_ktok"] = np.ascontiguousarray(np.stack([tokmajor(seq(k_, 0)), tokmajor(seq(k_, 1))]))
            m["ml_vtok"] = np.ascontiguousarray(np.stack([tokmajor(seq(v_, 0)), tokmajor(seq(v_, 1))]))
            g = np.zeros((2, 2, TT), np.float32)
            for d in range(2):
                for gi in range(2):
                    g[d, gi] = seq(GTf[d * 8 + gi * 4 + j][None, :], d)[0]
            m["ml_g"] = g
            m["ml_const"] = mlc
            maps.append(m)
    return maps

_PROGS = {}


def _prog(name):
    if name not in _PROGS:
        _PROGS[name] = {"a": build_phase_a, "b": build_phase_b, "c": build_phase_c}[name]()
    return _PROGS[name]


def phase_c_inputs(inp, l, x_full, xc_full, resA, resB):
    wG = np.ascontiguousarray(inp["w_in"][l][:, 2576:6672])
    wBr = np.ascontiguousarray(inp["w_branch"][l].reshape(1024, 1024))
    wOut = np.ascontiguousarray(inp["w_out"][l])
    wUp = np.ascontiguousarray(inp["ffn_up"][l])
    wDn = np.ascontiguousarray(inp["ffn_down"][l])
    fcw = np.zeros((128, 66), np.float32)
    for fb in range(22):
        for tap in range(3):
            fcw[:, fb * 3 + tap] = inp["ffn_conv_w"][l, tap, fb * 128:(fb + 1) * 128]
    fcb = np.ascontiguousarray(inp["ffn_conv_b"][l].reshape(22, 128).T)
    ngI = np.ascontiguousarray(np.concatenate([inp["norm_g"][l, i].reshape(8, 128).T for i in (1, 2, 3)], 1))
    mlnI = np.ascontiguousarray(inp["ml_norm"][l].reshape(2, 128).T)
    bd = np.zeros((128, 128), np.float32); bd[:64, :64] = 1; bd[64:, 64:] = 1
    maps = []
    for b in range(2):
        rb = resB[b * 4:(b + 1) * 4]
        dfs = [np.concatenate([np.asarray(r[nm]) for r in rb], 0) for nm in ("o_d", "o_f", "o_s")]
        brfull = np.concatenate(dfs, 0)
        om = [np.asarray(r["o_m"]) for r in rb]
        mf = np.concatenate([o[0] for o in om], 0); mb = np.concatenate([o[1] for o in om], 0)
        mf_lat, mf_ctx = mf[:, 256:], mf[:, :256]
        mb_lat, mb_ctx = mb[:, 256:][:, ::-1], mb[:, :256][:, ::-1]
        for q in range(4):
            i = b * 4 + q
            rA = resA[i]
            PT = np.asarray(rA["PT"])
            m = dict(
                xT=np.ascontiguousarray(to_cols(x_full[b], xc_full[b], q).T),
                hT=np.ascontiguousarray(np.asarray(rA["hT"])),
                modI=np.ascontiguousarray(np.asarray(rA["modO"])),
                brT=np.ascontiguousarray(to_cols(brfull[:, :8192].T, brfull[:, 8192:].T, q).T),
                mfT=np.ascontiguousarray(to_cols(mf_lat.T, mf_ctx.T, q).T),
                mbT=np.ascontiguousarray(to_cols(mb_lat.T, mb_ctx.T, q).T),
                oT=np.ascontiguousarray(np.concatenate([PT[(4 + j) * 128 + 64:(4 + j) * 128 + 128] for j in range(4)], 0)),
                wG=wG, wBr=wBr, wOut=wOut, wUp=wUp, wDn=wDn, fcw=fcw, fcb=fcb, ngI=ngI, mlnI=mlnI,
                cmask=cmask_for(q), bdI=bd)
            maps.append(m)
    return maps


def kernel(**inputs):
    inp = {k_: np.asarray(v, dtype=np.float32) for k_, v in inputs.items()}
    x_full = np.array(inp["x"], dtype=np.float32)
    xc_full = np.array(inp["ctx"], dtype=np.float32)
    cores = list(range(8))
    for l in range(4):
        resA = run_bass_kernel_spmd(_prog("a"), phase_a_inputs(inp, l, x_full, xc_full), core_ids=cores).results
        resB = run_bass_kernel_spmd(_prog("b"), phase_b_inputs(inp, l, resA), core_ids=cores).results
        resC = run_bass_kernel_spmd(_prog("c"), phase_c_inputs(inp, l, x_full, xc_full, resA, resB), core_ids=cores).results
        for i in range(8):
            b, q = i // 4, i % 4
            xo = np.asarray(resC[i]["xoT"])
            x_full[b, q * 2048:(q + 1) * 2048] = xo[:, 5:2053].T
            if q == 0:
                xc_full[b] = xo[:, 2058:2314].T
    return x_full.astype(np.float32)
```
